# Optimizing a Trainium2 kernel written in Bass

```python
import jax, jax.numpy as jnp
from jax import lax
import numpy as np

D_MODEL = 1024
BATCH = 2
SEQ = 16384
DEPTH = 1
DEC_BATCH = 128
DEC_SEQ = 8
PAST_LEN = 8192
PAGE_SIZE = 128

D_MIX = D_MODEL
HEAD_DIM = 64
ATT_WIDTH = D_MIX // 2
N_HEADS = ATT_WIDTH // HEAD_DIM
N_KV_HEADS = N_HEADS // 2
N_IDX_HEADS = 8
D_IDX = 64
TOPK_MAX = 256
GM_WIDTH = D_MIX - ATT_WIDTH
GM_GROUPS = 8
GM_GROUP_DIM = GM_WIDTH // GM_GROUPS
CHUNK = 128
Q_BLOCK = 128
PLE_DIM = 256
ROPE_THETA = 500000.0
ROPE_FRACTION = 4
EPS = 1e-6
SPLIT_SIZES = (N_HEADS * HEAD_DIM, N_KV_HEADS * HEAD_DIM, N_KV_HEADS * HEAD_DIM,
               N_IDX_HEADS * D_IDX, D_IDX, N_IDX_HEADS, ATT_WIDTH,
               GM_WIDTH, GM_WIDTH, GM_WIDTH)
D_IN = sum(SPLIT_SIZES)

kernel_name = "hybrid_dsa_gmlp_decoder_step"


def rmsnorm(x, g):
    x32 = x.astype(jnp.float32)
    y = x32 * lax.rsqrt(jnp.mean(x32 * x32, axis=-1, keepdims=True) + EPS)
    return y.astype(x.dtype) * g


def layernorm(x, g, b):
    x32 = x.astype(jnp.float32)
    xc = x32 - jnp.mean(x32, axis=-1, keepdims=True)
    y = xc * lax.rsqrt(jnp.mean(xc * xc, axis=-1, keepdims=True) + EPS)
    return y.astype(x.dtype) * g + b


def partial_rope(x, pos):
    rot = x.shape[-1] // ROPE_FRACTION
    half = rot // 2
    inv = jnp.power(ROPE_THETA, -jnp.arange(half, dtype=jnp.float32) * 2.0 / rot)
    ang = pos.astype(jnp.float32)[:, None] * inv[None, :]
    cos = jnp.cos(ang)[None, :, None, :]
    sin = jnp.sin(ang)[None, :, None, :]
    xr = x[..., :rot].astype(jnp.float32)
    x1, x2 = xr[..., :half], xr[..., half:]
    out = jnp.concatenate([x1 * cos - x2 * sin, x2 * cos + x1 * sin], axis=-1).astype(x.dtype)
    return jnp.concatenate([out, x[..., rot:]], axis=-1)


def chunk_spatial_gate(vn, w_s, b_s):
    B, T = vn.shape[:2]
    n_c = -(-T // CHUNK)
    pad = n_c * CHUNK - T
    vp = jnp.pad(vn, ((0, 0), (0, pad), (0, 0), (0, 0))).reshape(B, n_c, CHUNK, GM_GROUPS, GM_GROUP_DIM)
    causal = jnp.tril(jnp.ones((CHUNK, CHUNK), dtype=bool))
    ws = jnp.where(causal[None], w_s, 0)
    s = jnp.einsum('gij,bcjgd->bcigd', ws, vp) + b_s.T[None, None, :, :, None]
    return s.reshape(B, n_c * CHUNK, GM_GROUPS, GM_GROUP_DIM)[:, :T]


def project(x, pos, norm_g, w_in, qn_g, kn_g, ln_g, ln_b, w_s, b_s):
    B, T, _ = x.shape
    h = rmsnorm(x, norm_g)
    z = h @ w_in
    offs = np.cumsum(SPLIT_SIZES)[:-1].tolist()
    q, k, v, qi, ki, wi, ga, u, vm, gm = jnp.split(z, offs, axis=-1)
    q = partial_rope(rmsnorm(q.reshape(B, T, N_HEADS, HEAD_DIM), qn_g), pos)
    k = partial_rope(rmsnorm(k.reshape(B, T, N_KV_HEADS, HEAD_DIM), kn_g), pos)
    v = v.reshape(B, T, N_KV_HEADS, HEAD_DIM)
    qi = partial_rope(qi.reshape(B, T, N_IDX_HEADS, D_IDX), pos)
    ki = partial_rope(ki[:, :, None, :], pos)[:, :, 0, :]
    wi = wi * (N_IDX_HEADS * D_IDX) ** -0.5
    vn = layernorm(jax.nn.gelu(vm), ln_g, ln_b)
    s = chunk_spatial_gate(vn.reshape(B, T, GM_GROUPS, GM_GROUP_DIM), w_s, b_s).reshape(B, T, GM_WIDTH)
    gm_out = jax.nn.gelu(u) * s * jax.nn.silu(gm)
    return q, k, v, qi, ki, wi, ga, gm_out, vn


def indexer_scores(qi, wi, ki):
    dots = jnp.einsum('bthd,bld->bthl', qi, ki, preferred_element_type=jnp.float32)
    return jnp.einsum('bth,bthl->btl', wi.astype(jnp.float32), jax.nn.relu(dots))


def select_keys(score, q_pos, k_top):
    L = score.shape[-1]
    allowed = jnp.arange(L)[None, None, :] <= q_pos[None, :, None]
    score = jnp.where(allowed, score, -jnp.inf)
    _, idx = lax.top_k(score, k_top)
    valid = idx <= q_pos[None, :, None]
    return idx, valid


def sparse_attend(q, kg, vg, valid):
    B, T = q.shape[:2]
    qg = q.reshape(B, T, N_KV_HEADS, N_HEADS // N_KV_HEADS, HEAD_DIM)
    s = jnp.einsum('btgrd,btkgd->btgrk', qg, kg, preferred_element_type=jnp.float32) * HEAD_DIM ** -0.5
    s = jnp.where(valid[:, :, None, None, :], s, -jnp.inf)
    p = jax.nn.softmax(s, axis=-1).astype(vg.dtype)
    o = jnp.einsum('btgrk,btkgd->btgrd', p, vg)
    return o.reshape(B, T, N_HEADS, HEAD_DIM)


def prompt_attention(q, k, v, qi, ki, wi):
    B, S = q.shape[:2]
    nb = S // Q_BLOCK
    k_top = min(TOPK_MAX, S // 4)
    take = jax.vmap(lambda a, ii: a[ii])

    def blocks(a):
        return jnp.moveaxis(a.reshape((B, nb, Q_BLOCK) + a.shape[2:]), 1, 0)

    pos_blocks = jnp.arange(S).reshape(nb, Q_BLOCK)

    def one_block(args):
        qb, qib, wb, posb = args
        score = indexer_scores(qib, wb, ki)
        idx, valid = select_keys(score, posb, k_top)
        return sparse_attend(qb, take(k, idx), take(v, idx), valid)

    o = lax.map(one_block, (blocks(q), blocks(qi), blocks(wi), pos_blocks))
    return jnp.moveaxis(o, 0, 1).reshape(q.shape)


def sample_attention(q, k_new, v_new, qi, ki_new, wi, cache_k, cache_v, cache_idx_k, layer, page_table):
    DB, T = q.shape[:2]
    past = page_table.shape[1] * PAGE_SIZE
    L = past + T
    ki_past = cache_idx_k[layer, page_table].reshape(DB, past, D_IDX)
    ki_all = jnp.concatenate([ki_past, ki_new.astype(ki_past.dtype)], axis=1)
    q_pos = past + jnp.arange(T)
    score = indexer_scores(qi, wi, ki_all)
    idx, valid = select_keys(score, q_pos, min(TOPK_MAX, L // 4))
    is_past = (idx < past)[..., None, None]
    j = jnp.minimum(idx, past - 1)
    phys = jax.vmap(lambda pt, jj: pt[jj // PAGE_SIZE])(page_table, j)
    off = j % PAGE_SIZE
    jn = jnp.clip(idx - past, 0, T - 1)
    take = jax.vmap(lambda a, ii: a[ii])
    kg = jnp.where(is_past, cache_k[layer, phys, off], take(k_new, jn))
    vg = jnp.where(is_past, cache_v[layer, phys, off], take(v_new, jn))
    return sparse_attend(q, kg, vg, valid)


def finish(x, attn_o, ga, gm_out, w_out, p, ple_norm_g, w_ple_gate, w_ple_proj):
    B, T = x.shape[:2]
    mix = jnp.concatenate([attn_o.reshape(B, T, ATT_WIDTH) * jax.nn.silu(ga), gm_out], axis=-1)
    r = x + mix @ w_out
    gate = jax.nn.sigmoid(rmsnorm(r, ple_norm_g) @ w_ple_gate)
    return r + gate * (p @ w_ple_proj)


def setup_inputs(seed: int = 0) -> dict:
    key = jax.random.key(seed)
    ks = jax.random.split(key, 24)
    n_pages = PAST_LEN // PAGE_SIZE
    used = DEC_BATCH * n_pages
    n_pool = used + used // 4
    nrm = jax.random.normal
    f32 = jnp.float32
    page_table = jax.random.permutation(ks[0], n_pool)[:used].reshape(DEC_BATCH, n_pages).astype(jnp.int32)
    return {
        "x_prompt": nrm(ks[1], (BATCH, SEQ, D_MODEL), f32),
        "x_sample": nrm(ks[2], (DEC_BATCH, DEC_SEQ, D_MODEL), f32),
        "cache_k": nrm(ks[3], (DEPTH, n_pool, PAGE_SIZE, N_KV_HEADS, HEAD_DIM), f32),
        "cache_v": nrm(ks[4], (DEPTH, n_pool, PAGE_SIZE, N_KV_HEADS, HEAD_DIM), f32),
        "cache_idx_k": nrm(ks[5], (DEPTH, n_pool, PAGE_SIZE, D_IDX), f32),
        "page_table": page_table,
        "p_prompt": nrm(ks[6], (DEPTH, BATCH, SEQ, PLE_DIM), f32),
        "p_sample": nrm(ks[7], (DEPTH, DEC_BATCH, DEC_SEQ, PLE_DIM), f32),
        "norm_in_g": 1.0 + 0.01 * nrm(ks[8], (DEPTH, D_MODEL), f32),
        "w_in": nrm(ks[9], (DEPTH, D_MODEL, D_IN), f32) * D_MODEL ** -0.5,
        "q_norm_g": 1.0 + 0.01 * nrm(ks[10], (DEPTH, HEAD_DIM), f32),
        "k_norm_g": 1.0 + 0.01 * nrm(ks[11], (DEPTH, HEAD_DIM), f32),
        "ln_v_g": 1.0 + 0.01 * nrm(ks[12], (DEPTH, GM_WIDTH), f32),
        "ln_v_b": 0.01 * nrm(ks[13], (DEPTH, GM_WIDTH), f32),
        "w_s": nrm(ks[14], (DEPTH, GM_GROUPS, CHUNK, CHUNK), f32) * 0.5 * CHUNK ** -0.5,
        "b_s": 1.0 + 0.01 * nrm(ks[15], (DEPTH, GM_GROUPS, CHUNK), f32),
        "w_out": nrm(ks[16], (DEPTH, D_MIX, D_MODEL), f32) * D_MIX ** -0.5,
        "ple_norm_g": 1.0 + 0.01 * nrm(ks[17], (DEPTH, D_MODEL), f32),
        "w_ple_gate": nrm(ks[18], (DEPTH, D_MODEL, D_MODEL), f32) * D_MODEL ** -0.5,
        "w_ple_proj": nrm(ks[19], (DEPTH, PLE_DIM, D_MODEL), f32) * PLE_DIM ** -0.5,
    }


def reference(x_prompt, x_sample, cache_k, cache_v, cache_idx_k, page_table, p_prompt, p_sample,
              norm_in_g, w_in, q_norm_g, k_norm_g, ln_v_g, ln_v_b, w_s, b_s, w_out,
              ple_norm_g, w_ple_gate, w_ple_proj):
    S = x_prompt.shape[1]
    T = x_sample.shape[1]
    past = page_table.shape[1] * PAGE_SIZE
    pos_p = jnp.arange(S)
    pos_s = past + jnp.arange(T)
    hp, hs = x_prompt, x_sample
    kp, vp, ikp, ksl, vsl, iks, gvs = [], [], [], [], [], [], []
    for i in range(DEPTH):
        lw = (norm_in_g[i], w_in[i], q_norm_g[i], k_norm_g[i], ln_v_g[i], ln_v_b[i], w_s[i], b_s[i])
        q, k, v, qi, ki, wi, ga, gm, vn = project(hp, pos_p, *lw)
        o = prompt_attention(q, k, v, qi, ki, wi)
        hp = finish(hp, o, ga, gm, w_out[i], p_prompt[i], ple_norm_g[i], w_ple_gate[i], w_ple_proj[i])
        kp.append(k)
        vp.append(v)
        ikp.append(ki)
        q, k, v, qi, ki, wi, ga, gm, vn = project(hs, pos_s, *lw)
        o = sample_attention(q, k, v, qi, ki, wi, cache_k, cache_v, cache_idx_k, i, page_table)
        hs = finish(hs, o, ga, gm, w_out[i], p_sample[i], ple_norm_g[i], w_ple_gate[i], w_ple_proj[i])
        ksl.append(k)
        vsl.append(v)
        iks.append(ki)
        gvs.append(vn)
    return (hp, hs, jnp.stack(kp), jnp.stack(vp), jnp.stack(ikp),
            jnp.stack(ksl), jnp.stack(vsl), jnp.stack(iks), jnp.stack(gvs))
```

```python
import contextlib
import numpy as np
import ml_dtypes
import concourse.bass as bass
import concourse.mybir as mybir
from concourse.bass_utils import run_bass_kernel_spmd

F32 = mybir.dt.float32
BF16 = mybir.dt.bfloat16
I32 = mybir.dt.int32
ALU = mybir.AluOpType
AF = mybir.ActivationFunctionType
AX = mybir.AxisListType

D = 1024
DIN = 3656
SEQ = 16384
NBLK = 128
NOWN = 32
NPOOL = 10240
NPG = 64
PAST = 8192
NEG = -1.0e30
EPS = 1e-6
NR = 18
TOPK = 256
LS = PAST + 128


class Sched:
    def __init__(self, nc, es):
        self.nc = nc
        self.es = es
        self.eng = {'pe': nc.tensor, 'act': nc.scalar, 'dve': nc.vector, 'pool': nc.gpsimd, 'sp': nc.sync}
        self.sem = {}
        self.cnt = {}
        self.seen = {e: {} for e in self.eng}
        self.last_w = {}
        self.readers = {}
        for e in ('pe', 'act', 'dve', 'pool'):
            self._mksem(e)

    def _mksem(self, k):
        self.sem[k] = self.es.enter_context(self.nc.semaphore("s_%d" % len(self.sem)))
        self.cnt[k] = 0

    def _deps(self, e, r, w):
        need = {}

        def add(ev):
            if ev is not None:
                need[ev[0]] = max(need.get(ev[0], 0), ev[1])
        for b in r:
            add(self.last_w.get(b))
        for b in w:
            add(self.last_w.get(b))
            for ev in self.readers.get(b, ()):
                add(ev)
        for k, v in need.items():
            if self.seen[e].get(k, 0) < v:
                self.eng[e].wait_ge(self.sem[k], v)
                self.seen[e][k] = v

    def _record(self, ev, r, w):
        for b in r:
            lst = self.readers.setdefault(b, [])
            lst[:] = [x for x in lst if x[0] != ev[0]]
            lst.append(ev)
        for b in w:
            self.last_w[b] = ev
            self.readers[b] = []

    def op(self, e, fn, r=(), w=()):
        self._deps(e, r, w)
        ins = fn(self.eng[e])
        self.cnt[e] += 1
        ins.then_inc(self.sem[e], 1)
        self._record((e, self.cnt[e]), r, w)
        return ins

    def dma(self, q, semkey, fn, r=(), w=()):
        k = ('d', semkey)
        if k not in self.sem:
            self._mksem(k)
        self._deps(q, r, w)
        ins = fn(self.eng[q])
        self.cnt[k] += 16
        ins.then_inc(self.sem[k], 16)
        self._record((k, self.cnt[k]), r, w)
        return ins

    def wait_all(self, e='sp'):
        for k, v in self.cnt.items():
            if v > 0 and self.seen[e].get(k, 0) < v:
                self.eng[e].wait_ge(self.sem[k], v)
                self.seen[e][k] = v

    def barrier(self):
        for e in self.eng:
            self.wait_all(e)


def build():
    nc = bass.Bass("TRN2", target_bir_lowering=False)

    def din(name, shape, dt=F32):
        return nc.dram_tensor(name, list(shape), dt, kind="ExternalInput").ap()

    def dout(name, shape, dt=F32):
        return nc.dram_tensor(name, list(shape), dt, kind="ExternalOutput").ap()

    def dscr(name, shape, dt):
        return nc.dram_tensor(name, list(shape), dt, kind="Internal").ap()

    xb = din("xb", [SEQ, D])
    xown = din("xown", [NOWN * 128, D])
    pown = din("pown", [NOWN * 128, 256])
    xs = din("xs", [128, D])
    ps_d = din("ps", [128, 256])
    ck = din("ck", [NPOOL * 128, 256])
    cv = din("cv", [NPOOL * 128, 256])
    cik = din("cik", [NPOOL * 128, 64])
    pt_d = din("pt", [16, NPG], I32)
    w_in = din("w_in", [D, DIN])
    w_out = din("w_out", [D, D])
    w_gate = din("w_gate", [D, D])
    w_proj = din("w_proj", [256, D])
    gin_col = din("gin_col", [128, 8])
    gple_col = din("gple_col", [128, 8])
    gq_d = din("gq", [1, 64])
    gk_d = din("gk", [1, 64])
    lng_d = din("lng", [1, 512])
    lnb_d = din("lnb", [1, 512])
    ws_d = din("ws", [128, 8, 128])
    wsS_d = din("wsS", [128, 8, 128])
    bsT_d = din("bsT", [128, 8])
    bsS_d = din("bsS", [128, 8])
    identb_d = din("identb", [128, 128], BF16)
    tril_d = din("tril", [128, 128])
    cmask_d = din("cmask", [128, 512])
    newneg_d = din("newneg", [128, 128])
    cosK_d = din("cosK", [128, NBLK, 8])
    sinK_d = din("sinK", [128, NBLK, 8])
    cosQ_d = din("cosQ", [128, NOWN, 8])
    sinQ_d = din("sinQ", [128, NOWN, 8])
    cosS_d = din("cosS", [128, 8])
    sinS_d = din("sinS", [128, 8])
    selb_d = din("selb", [16, 248])
    m01_d = din("m01", [16, 2])
    pow2_d = din("pow2", [128, NR + 2])
    piof_d = din("piof", [128, 1])

    y_own = dout("y_own", [NOWN * 128, D])
    k_own = dout("k_own", [NOWN * 128, 256])
    v_own = dout("v_own", [NOWN * 128, 256])
    ki_own = dout("ki_own", [NOWN * 128, 64])
    ys = dout("ys", [128, D])
    ks = dout("ks", [128, 256])
    vs = dout("vs", [128, 256])
    kis = dout("kis", [128, 64])
    vns = dout("vns", [128, 512])

    KT_s = dscr("KT_s", [128, 2, SEQ], BF16)
    V_s = dscr("V_s", [128, NBLK, 260], BF16)
    Q_s = dscr("Q_s", [NOWN, 128, 1024], BF16)
    QI_s = dscr("QI_s", [NOWN, 128, 1024], BF16)
    SG_s = dscr("SG_s", [NOWN, 128, 8], F32)
    GA_s = dscr("GA_s", [NOWN, 128, 512], F32)
    GM_s = dscr("GM_s", [NOWN, 128, 512], BF16)

    with contextlib.ExitStack() as es:
        S = Sched(nc, es)

        def sb(name, shape, dt=F32):
            return es.enter_context(nc.sbuf_tensor("sb_" + name, list(shape), dt))

        big = sb("big", [128, 16640], F32)
        kiT = sb("kiT", [128, SEQ], BF16)
        R2 = sb("R2", [128, 9216], F32)
        psf = es.enter_context(nc.psum_tensor("psf", [128, 6, 512], F32))
        psb = es.enter_context(nc.psum_tensor("psb", [128, 2, 1024], BF16))

        win = big[:].bitcast(BF16)[:, 0:8 * DIN].rearrange("p (c n) -> p c n", c=8)
        scores = big
        qiTzS = big[:].bitcast(BF16)[:, 2 * LS:2 * LS + 16384].rearrange("p (s h q) -> p s h q", s=16, h=8)
        kiTc = kiT[:, :].rearrange("p (s n) -> p s n", s=32)
        stage = R2[:, 0:DIN]
        cosK = R2[:, 3656:3656 + 1024].rearrange("p (b e) -> p b e", e=8)
        sinK = R2[:, 4680:4680 + 1024].rearrange("p (b e) -> p b e", e=8)
        cosQ = R2[:, 5704:5704 + 256].rearrange("p (b e) -> p b e", e=8)
        sinQ = R2[:, 5960:5960 + 256].rearrange("p (b e) -> p b e", e=8)
        R2b = R2[:].bitcast(BF16)
        wout = R2b[:, 0:8192].rearrange("p (c n) -> p c n", c=8)
        wgate = R2b[:, 8192:16384].rearrange("p (c n) -> p c n", c=8)
        wproj = R2b[:, 16384:18432].rearrange("p (c n) -> p c n", c=2)

        xt = [sb("xt%d" % i, [128, D]) for i in range(2)]
        gate = sb("gate", [128, D])
        hb = sb("hb", [128, D], BF16)
        hT = sb("hT", [128, 8, 128], BF16)
        mix = sb("mix", [128, D], BF16)
        tp = {n: sb("t_" + n, [128, 512]) for n in "ABCDFGH"}
        tp['E'] = tp['D']
        qb = sb("qb", [128, 512], BF16)
        qib = sb("qib", [128, 512], BF16)
        vnb = sb("vnb", [128, 512], BF16)
        kb16 = sb("kb16", [128, 256], BF16)
        kidup = sb("kidup", [128, 4, 128], BF16)
        QTz = sb("QTz", [128, 1024], BF16)
        qiTz = sb("qiTz", [128, 1024], BF16)
        sgn = sb("sgn", [128, 8])
        KTn = sb("KTn", [128, 2, 128], BF16)
        Vn = sb("Vn", [128, 260], BF16)
        kiTn = sb("kiTn", [128, 128], BF16)
        PT = [sb("PT%d" % i, [128, 1040], BF16) for i in range(2)]
        ktb4 = PT[0][:, 0:1024].rearrange("p (c n) -> p c n", c=2)
        vaug4 = PT[1][:, 0:1040].rearrange("p (u n) -> p u n", u=4)
        KTc = [sb("KTc%d" % i, [128, 2, 512], BF16) for i in range(2)]
        Vc = [sb("Vc%d" % i, [128, 4, 260], BF16) for i in range(2)]
        mk = [sb("mk%d" % i, [128, 128], BF16) for i in range(2)]
        junk = sb("junk", [128, 1024], BF16)
        ptile = sb("ptile", [128, 256])
        pb16 = sb("pb16", [128, 256], BF16)
        pT = sb("pT", [128, 2, 128], BF16)
        sm = sb("sm", [128, 64])
        rp = sb("rp", [128, 6, 64])
        ident = sb("ident", [128, 128], BF16)
        tril = sb("tril", [128, 128])
        cmask = sb("cmask", [128, 512])
        newneg = sb("newneg", [128, 128])
        gq = sb("gq", [128, 64])
        gk = sb("gk", [128, 64])
        lng = sb("lng", [128, 512])
        lnb = sb("lnb", [128, 512])
        bsT = sb("bsT", [128, 8])
        bsS = sb("bsS", [128, 8])
        wsT = sb("wsT", [128, 8, 128], BF16)
        wsTS = sb("wsTS", [128, 8, 128], BF16)
        cosS = sb("cosS", [128, 8])
        sinS = sb("sinS", [128, 8])
        selb = sb("selb", [16, 248])
        m01 = sb("m01", [16, 2])
        pow2 = sb("pow2", [128, NR + 2])
        piof = sb("piof", [128, 1])
        gcol = sb("gcol", [128, 16])
        steps = sb("steps", [128, NR + 2])
        bis = sb("bis", [128, 8])
        ptb = xt[1][:, :].bitcast(I32).rearrange("p (s n) -> p s n", s=16)
        idxall = gate[:, :].bitcast(I32).rearrange("p (s n) -> p s n", s=16)
        kraw = [hb[:, :].bitcast(F32)[:, i * 256:(i + 1) * 256].rearrange("p (u d) -> p u d", u=4) for i in range(2)]
        kraw2 = [tp['D'][:, i * 256:(i + 1) * 256] for i in range(2)]
        vraw = [tp['G'][:, i * 256:(i + 1) * 256] for i in range(2)]
        KTp = [KTc[i][:, :, 0:128] for i in range(2)]
        Vp = [Vc[i][:, 0, :] for i in range(2)]
        PTs = [sb("PTs%d" % i, [128, 64], BF16) for i in range(2)]
        On = tp['C'][0:16, 0:256]
        On2 = tp['H'][0:16, :]

        def act(out, in_, func, r, w, **kw):
            return S.op('act', lambda e: e.activation(out=out, in_=in_, func=func, **kw), r=r, w=w)

        def ts(eng, out, in0, s1, s2, op0, op1, r, w, accum_out=None):
            if op1 is None:
                return S.op(eng, lambda e: e.tensor_scalar(out=out, in0=in0, scalar1=s1, scalar2=None, op0=op0), r=r, w=w)
            if accum_out is not None:
                return S.op(eng, lambda e: e.tensor_scalar(out=out, in0=in0, scalar1=s1, scalar2=s2, op0=op0, op1=op1, accum_out=accum_out), r=r, w=w)
            return S.op(eng, lambda e: e.tensor_scalar(out=out, in0=in0, scalar1=s1, scalar2=s2, op0=op0, op1=op1), r=r, w=w)

        def tt(eng, out, in0, in1, op, r, w):
            return S.op(eng, lambda e: e.tensor_tensor(out=out, in0=in0, in1=in1, op=op), r=r, w=w)

        def stt(out, in0, scalar, in1, op0, op1, r, w):
            return S.op('dve', lambda e: e.scalar_tensor_tensor(out=out, in0=in0, scalar=scalar, in1=in1, op0=op0, op1=op1), r=r, w=w)

        def mm(out, lhsT, rhs, start, stop, r, w):
            return S.op('pe', lambda e: e.matmul(out, lhsT, rhs, start=start, stop=stop, skip_group_check=True), r=r, w=w)

        def tr(out, in_, r, w, idn=None):
            return S.op('pe', lambda e: e.transpose(out=out, in_=in_, identity=ident[:, :] if idn is None else idn), r=list(r) + ['ident'], w=w)

        def ld(semkey, out, in_, w, r=(), q='sp'):
            return S.dma(q, semkey, lambda e: e.dma_start(out=out, in_=in_), r=r, w=w)

        def st(semkey, out, in_, r, w=()):
            return S.dma('sp', semkey, lambda e: e.dma_start(out=out, in_=in_), r=r, w=w)

        def bc(ap, axis, shape):
            return ap.unsqueeze(axis).to_broadcast(list(shape))

        for name, t_, d_ in [("ident", ident, identb_d), ("tril", tril, tril_d), ("cmask", cmask, cmask_d),
                             ("newneg", newneg, newneg_d), ("bsT", bsT, bsT_d), ("bsS", bsS, bsS_d),
                             ("cosS", cosS, cosS_d), ("sinS", sinS, sinS_d), ("selb", selb, selb_d),
                             ("m01", m01, m01_d), ("pow2", pow2, pow2_d), ("piof", piof, piof_d)]:
            ld(name, t_[:], d_, w=[name])
        ld("gcol", gcol[:, 0:8], gin_col, w=["gcol"])
        ld("gcol", gcol[:, 8:16], gple_col, w=["gcol"])
        ld("gq", gq[:], gq_d.partition_broadcast(128), w=["gq"])
        ld("gk", gk[:], gk_d.partition_broadcast(128), w=["gk"])
        ld("lng", lng[:], lng_d.partition_broadcast(128), w=["lng"])
        ld("lnb", lnb[:], lnb_d.partition_broadcast(128), w=["lnb"])
        ld("R2", cosK, cosK_d, w=["cosK"])
        ld("R2", sinK, sinK_d, w=["cosK"])
        ld("R2", cosQ, cosQ_d, w=["cosK"])
        ld("R2", sinQ, sinQ_d, w=["cosK"])
        S.op('pool', lambda e: e.memset(QTz[:], 0.0), w=['QTz'])
        S.op('pool', lambda e: e.memset(qiTz[:], 0.0), w=['qiTz'])
        S.op('pool', lambda e: e.memset(vaug4, 1.0), w=['vaug4'])
        S.op('pool', lambda e: e.memset(Vn[:], 1.0), w=['Vn'])
        for (src_d, dst, nm) in ((ws_d, wsT, 'wsT'), (wsS_d, wsTS, 'wsTS')):
            for g in range(8):
                ld("tA", tp['A'][:, 0:128], src_d[:, g, :], w=['tA'])
                tt('dve', qb[:, 0:128], tp['A'][:, 0:128], tril[:, :], ALU.mult, r=['tA', 'tril'], w=['qb'])
                tr(psb[:, 0, 0:128], qb[:, 0:128], r=['qb'], w=['b0'])
                act(dst[:, g, :], psb[:, 0, 0:128], AF.Copy, r=['b0'], w=[nm])
        for c in range(8):
            ld("stage", stage, w_in[c * 128:(c + 1) * 128, :], w=['stage'])
            for G_ in range(2):
                ts('dve', win[:, c, G_ * 256:(G_ + 1) * 256].rearrange("p (r e d) -> p r e d", r=2, e=2),
                   stage[:, G_ * 256:(G_ + 1) * 256].rearrange("p (e r d) -> p r e d", e=2, r=2),
                   gcol[:, c:c + 1], None, ALU.mult, None, r=['stage', 'gcol'], w=['win'])
            act(win[:, c, 512:2048], stage[:, 512:2048], AF.Copy, r=['stage', 'gcol'], w=['win'], scale=gcol[:, c:c + 1])
            ts('pool', win[:, c, 2048:DIN], stage[:, 2048:DIN], gcol[:, c:c + 1], None, ALU.mult, None, r=['stage', 'gcol'], w=['win'])

        def norm_hT(xtile, xkey, wkey_suffix=''):
            act(junk[:, 0:1024], xtile, AF.Square, r=[xkey], w=['junk', 'sm0'], accum_out=sm[:, 0:1])
            act(sm[:, 1:2], sm[:, 0:1], AF.Sqrt, r=['sm0'], w=['sm1'], scale=1.0 / D, bias=epsb[:, 0:1])
            S.op('dve', lambda e: e.reciprocal(out=sm[:, 2:3], in_=sm[:, 1:2]), r=['sm1'], w=['sm2'])
            ts('dve', hb[:, :], xtile, sm[:, 2:3], None, ALU.mult, None, r=[xkey, 'sm2'], w=['hb'])
            for c in range(8):
                tr(psb[:, 0, c * 128:(c + 1) * 128], hb[:, c * 128:(c + 1) * 128], r=['hb'], w=['b0'])
            act(hT[:].rearrange("p c t -> p (c t)"), psb[:, 0, :], AF.Copy, r=['b0'], w=['hT'])

        def proj(bank, col0, n, wt=None, wkey='win'):
            wt_ = win if wt is None else wt
            for c in range(8):
                mm(psf[:, bank, 0:n], hT[:, c, :], wt_[:, c, col0:col0 + n], c == 0, c == 7,
                   r=['hT', wkey], w=[('f', bank)])

        def norm_rope(src, srckey, H, gt, gkey, cos, sin, cskey, dst, dstkey, do_norm):
            W = H * 64
            src3 = src.rearrange("p (h d) -> p h d", d=64)
            dst3 = dst.rearrange("p (h d) -> p h d", d=64)
            if do_norm:
                act(tp['A'][:, 0:W], src, AF.Square, r=[srckey], w=['tA'])
                S.op('dve', lambda e: e.tensor_reduce(out=sm[:, 8:8 + H], in_=tp['A'][:, 0:W].rearrange("p (h d) -> p h d", d=64),
                                                      axis=AX.X, op=ALU.add), r=['tA'], w=['sm8'])
                act(sm[:, 16:16 + H], sm[:, 8:8 + H], AF.Sqrt, r=['sm8'], w=['sm16'], scale=1.0 / 64, bias=epsb[:, 0:1])
                S.op('dve', lambda e: e.reciprocal(out=sm[:, 24:24 + H], in_=sm[:, 16:16 + H]), r=['sm16'], w=['sm24'])
                tt('dve', dst3, src3, bc(sm[:, 24:24 + H], 2, [128, H, 64]), ALU.mult, r=[srckey, 'sm24'], w=[dstkey])
                tt('dve', dst3, dst3, bc(gt[:, :], 1, [128, H, 64]), ALU.mult, r=[dstkey, gkey], w=[dstkey])
            else:
                act(dst, src, AF.Copy, r=[srckey], w=[dstkey])
            x1 = dst3[:, :, 0:8]
            x2 = dst3[:, :, 8:16]
            cb = bc(cos, 1, [128, H, 8])
            sb_ = bc(sin, 1, [128, H, 8])
            ra = rp[:, 0, 0:H * 8].rearrange("p (h e) -> p h e", e=8)
            rb = rp[:, 1, 0:H * 8].rearrange("p (h e) -> p h e", e=8)
            rc = rp[:, 2, 0:H * 8].rearrange("p (h e) -> p h e", e=8)
            rd = rp[:, 3, 0:H * 8].rearrange("p (h e) -> p h e", e=8)
            tt('dve', ra, x1, cb, ALU.mult, r=[dstkey, cskey], w=['rp'])
            tt('dve', rb, x2, sb_, ALU.mult, r=[dstkey, cskey], w=['rp'])
            tt('dve', rc, x2, cb, ALU.mult, r=[dstkey, cskey], w=['rp'])
            tt('dve', rd, x1, sb_, ALU.mult, r=[dstkey, cskey], w=['rp'])
            tt('dve', x1, ra, rb, ALU.subtract, r=['rp'], w=[dstkey])
            tt('dve', x2, rc, rd, ALU.add, r=['rp'], w=[dstkey])

        epsb = sb("epsb", [128, 1])
        S.op('pool', lambda e: e.memset(epsb[:], EPS), w=['epsb'])

        def kv_post(cos, sin, cskey, k_out, v_out, ki_out, kt_dst, kt_key, v_dst, v_key, kiT_dst, kiT_key, bkv=0, bki=1):
            norm_rope(psf[:, bkv, 0:256], ('f', bkv), 4, gk, 'gk', cos, sin, cskey, tp['B'][:, 0:256], 'tB', True)
            if k_out is not None:
                st('tB', k_out, tp['B'][:, 0:256], r=['tB'])
            act(kb16[:, :], tp['B'][:, 0:256], AF.Copy, r=['tB'], w=['kb16'])
            for kc in range(2):
                tr(psb[:, 1, kc * 128:(kc + 1) * 128], kb16[:, kc * 128:(kc + 1) * 128], r=['kb16'], w=[('b1', 0), ('b1', 1)])
            act(kt_dst, psb[:, 1, 0:256].rearrange("p (c t) -> p c t", c=2), AF.Copy, r=[('b1', 0), ('b1', 1)], w=[kt_key])
            if v_out is not None:
                act(tp['B'][:, 256:512], psf[:, bkv, 256:512], AF.Copy, r=[('f', bkv)], w=['tBv'])
                st('tBv', v_out, tp['B'][:, 256:512], r=['tBv'])
            S.op('dve', lambda e: e.tensor_copy(out=v_dst.rearrange("p (g d) -> p g d", d=65)[:, :, 0:64],
                                                in_=psf[:, bkv, 256:512].rearrange("p (g d) -> p g d", d=64)),
                 r=[('f', bkv)], w=[v_key])
            norm_rope(psf[:, bki, 0:64], ('f', bki), 1, None, None, cos, sin, cskey, tp['C'][:, 0:64], 'tC', False)
            if ki_out is not None:
                st('tC', ki_out, tp['C'][:, 0:64], r=['tC'])
            S.op('dve', lambda e: e.tensor_copy(out=kidup[:, 0, :].rearrange("p (a d) -> p a d", a=2),
                                                in_=bc(tp['C'][:, 0:64], 1, [128, 2, 64])), r=['tC'], w=['kidup'])
            tr(psb[:, 1, 256:384], kidup[:, 0, :], r=['kidup'], w=['b1r'])
            act(kiT_dst, psb[:, 1, 256:384], AF.Copy, r=['b1r'], w=[kiT_key])

        def q_post(cos, sin, cskey, ws_T, wsk, bs_t, bsk, vn_out, gm_dst, gm_key):
            norm_rope(psf[:, 2, :], ('f', 2), 8, gq, 'gq', cos, sin, cskey, tp['D'][:, :], 'tD', True)
            act(qb[:, :], tp['D'][:, :], AF.Copy, r=['tD'], w=['qb'])
            for c in range(4):
                tr(psb[:, 1, 512 + c * 128:512 + (c + 1) * 128], qb[:, c * 128:(c + 1) * 128], r=['qb'], w=['b1q'])
            QTv = QTz[:, :].rearrange("p (G e r q) -> p G e r q", G=2, e=2, r=2)
            pv = psb[:, 1, 512:1024].rearrange("p (G r q) -> p G r q", G=2, r=2)
            act(QTv[0:64, :, 0, :, :], pv[0:64], AF.Copy, r=['b1q'], w=['QTz'])
            S.op('dve', lambda e: e.tensor_copy(out=QTv[64:128, :, 1, :, :], in_=pv[64:128]), r=['b1q'], w=['QTz'])
            norm_rope(psf[:, 3, :], ('f', 3), 8, None, None, cos, sin, cskey, tp['E'][:, :], 'tD', False)
            act(sm[:, 32:40], psf[:, 1, 64:72], AF.Abs, r=[('f', 1)], w=['sm32'], scale=512.0 ** -0.5)
            ts('dve', sgn[:, :], psf[:, 1, 64:72], 0.0, 2.0, ALU.is_ge, ALU.mult, r=[('f', 1)], w=['sgn'])
            ts('dve', sgn[:, :], sgn[:, :], -1.0, None, ALU.add, None, r=['sgn'], w=['sgn'])
            tt('dve', qib[:, :].rearrange("p (h d) -> p h d", d=64), tp['E'][:, :].rearrange("p (h d) -> p h d", d=64),
               bc(sm[:, 32:40], 2, [128, 8, 64]), ALU.mult, r=['tD', 'sm32'], w=['qib'])
            for c in range(4):
                tr(psb[:, 0, c * 128:(c + 1) * 128], qib[:, c * 128:(c + 1) * 128], r=['qib'], w=['b0'])
            qiv = qiTz[:, :].rearrange("p (c e q) -> p c e q", c=4, e=2)
            pv2 = psb[:, 0, 0:512].rearrange("p (c q) -> p c q", c=4)
            act(qiv[0:64, :, 0, :], pv2[0:64], AF.Copy, r=['b0'], w=['qiTz'])
            S.op('dve', lambda e: e.tensor_copy(out=qiv[64:128, :, 1, :], in_=pv2[64:128]), r=['b0'], w=['qiTz'])
            act(tp['F'][:, :], psf[:, 4, :], AF.Silu, r=[('f', 4)], w=['tF'])
            act(tp['G'][:, :], psf[:, 5, :], AF.Gelu_apprx_tanh, r=[('f', 5)], w=['tG'])
            proj(0, 2632, 512)
            proj(1, 3144, 512)
            act(tp['H'][:, :], psf[:, 0, :], AF.Gelu_apprx_tanh, r=[('f', 0)], w=['tH', 'sm40'], accum_out=sm[:, 40:41])
            act(junk[:, 0:512], tp['H'][:, :], AF.Square, r=['tH'], w=['junk', 'sm41'], accum_out=sm[:, 41:42])
            ts('dve', sm[:, 42:43], sm[:, 40:41], 1.0 / 512, None, ALU.mult, None, r=['sm40'], w=['sm42'])
            tt('dve', sm[:, 43:44], sm[:, 42:43], sm[:, 42:43], ALU.mult, r=['sm42'], w=['sm43'])
            stt(sm[:, 44:45], sm[:, 41:42], 1.0 / 512, sm[:, 43:44], ALU.mult, ALU.subtract, r=['sm41', 'sm43'], w=['sm44'])
            act(sm[:, 45:46], sm[:, 44:45], AF.Sqrt, r=['sm44'], w=['sm45'], scale=1.0, bias=epsb[:, 0:1])
            S.op('dve', lambda e: e.reciprocal(out=sm[:, 46:47], in_=sm[:, 45:46]), r=['sm45'], w=['sm46'])
            ts('dve', tp['H'][:, :], tp['H'][:, :], sm[:, 42:43], sm[:, 46:47], ALU.subtract, ALU.mult, r=['tH', 'sm42', 'sm46'], w=['tH'])
            tt('dve', tp['H'][:, :], tp['H'][:, :], lng[:, :], ALU.mult, r=['tH', 'lng'], w=['tH'])
            tt('pool', tp['H'][:, :], tp['H'][:, :], lnb[:, :], ALU.add, r=['tH', 'lnb'], w=['tH'])
            if vn_out is not None:
                st('tH', vn_out, tp['H'][:, :], r=['tH'])
            act(vnb[:, :], tp['H'][:, :], AF.Copy, r=['tH'], w=['vnb'])
            for g in range(8):
                mm(psf[:, 2, g * 64:(g + 1) * 64], ws_T[:, g, :], vnb[:, g * 64:(g + 1) * 64], True, True,
                   r=['vnb', wsk], w=[('f', 2)])
            tt('dve', tp['A'][:, :].rearrange("p (g d) -> p g d", d=64), psf[:, 2, :].rearrange("p (g d) -> p g d", d=64),
               bc(bs_t[:, :], 2, [128, 8, 64]), ALU.add, r=[('f', 2), bsk], w=['tA'])
            act(tp['C'][:, :], psf[:, 1, :], AF.Silu, r=[('f', 1)], w=['tC'])
            tt('dve', tp['A'][:, :], tp['A'][:, :], tp['G'][:, :], ALU.mult, r=['tA', 'tG'], w=['tA'])
            tt('dve', gm_dst, tp['A'][:, :], tp['C'][:, :], ALU.mult, r=['tA', 'tC'], w=[gm_key])

        def full_proj():
            proj(0, 512, 512)
            proj(1, 1536, 72)
            proj(2, 0, 512)
            proj(3, 1024, 512)
            proj(4, 1608, 512)
            proj(5, 2120, 512)

        def p1a_front(blk):
            x_ = xt[blk % 2]
            xk = ('xt', blk % 2)
            ld(xk, x_[:, :], xb[blk * 128:(blk + 1) * 128, :], w=[xk])
            norm_hT(x_[:, :], xk)
            proj(2 * (blk % 2), 512, 512)
            proj(2 * (blk % 2) + 1, 1536, 64)

        def p1a_back(blk):
            u = blk % 4
            kv_post(cosK[:, blk, :], sinK[:, blk, :], 'cosK', None, None, None,
                    ktb4[:, :, u * 128:(u + 1) * 128], 'ktb4', vaug4[:, u, :], 'vaug4',
                    kiT[:, blk * 128:(blk + 1) * 128], ('kiT', blk // 4), bkv=2 * (blk % 2), bki=2 * (blk % 2) + 1)
            if u == 3:
                c4 = blk // 4
                st('ktb4', KT_s[:, :, c4 * 512:(c4 + 1) * 512], ktb4, r=['ktb4'], w=[('KTs', c4)])
                st('vaug4', V_s[:, c4 * 4:(c4 + 1) * 4, :], vaug4, r=['vaug4'], w=[('Vs', c4)])

        p1a_front(0)
        for blk in range(1, NBLK):
            p1a_front(blk)
            p1a_back(blk - 1)
        p1a_back(NBLK - 1)

        for i in range(NOWN):
            x_ = xt[i % 2]
            xk = ('xt', i % 2)
            ld(xk, x_[:, :], xown[i * 128:(i + 1) * 128, :], w=[xk])
            norm_hT(x_[:, :], xk)
            full_proj()
            rows = slice(i * 128, (i + 1) * 128)
            kv_post(cosQ[:, i, :], sinQ[:, i, :], 'cosK', k_own[rows, :], v_own[rows, :], ki_own[rows, :],
                    KTn[:, :, :], 'KTn', Vn[:, :], 'Vn', kiTn[:, :], 'kiTn')
            q_post(cosQ[:, i, :], sinQ[:, i, :], 'cosK', wsT, 'wsT', bsT, 'bsT', None, mix[:, 512:1024], 'mixg')
            st('QTz', Q_s[i], QTz[:, :], r=['QTz'], w=[('Qs', i)])
            st('qiTz', QI_s[i], qiTz[:, :], r=['qiTz'], w=[('QIs', i)])
            st('sgn', SG_s[i], sgn[:, :], r=['sgn'], w=[('SGs', i)])
            st('tF', GA_s[i], tp['F'][:, :], r=['tF'], w=[('GAs', i)])
            st('mixg', GM_s[i], mix[:, 512:1024], r=['mixg'], w=[('GMs', i)])

        S.barrier()

        for c in range(8):
            ld("tA", tp['A'][:, :], w_out[c * 128:(c + 1) * 128, 0:512], w=['tA'])
            ld("tB", tp['B'][:, :], w_out[c * 128:(c + 1) * 128, 512:1024], w=['tB'])
            act(wout[:, c, 0:512], tp['A'][:, :], AF.Copy, r=['tA'], w=['wout'])
            S.op('dve', lambda e, c=c: e.tensor_copy(out=wout[:, c, 512:1024], in_=tp['B'][:, :]), r=['tB'], w=['wout'])
            ld("tC", tp['C'][:, :], w_gate[c * 128:(c + 1) * 128, 0:512], w=['tC'])
            ld("tD", tp['D'][:, :], w_gate[c * 128:(c + 1) * 128, 512:1024], w=['tD'])
            act(wgate[:, c, 0:512], tp['C'][:, :], AF.Copy, r=['tC', 'gcol'], w=['wgate'], scale=gcol[:, 8 + c:9 + c])
            ts('dve', wgate[:, c, 512:1024], tp['D'][:, :], gcol[:, 8 + c:9 + c], None, ALU.mult, None, r=['tD', 'gcol'], w=['wgate'])
        for c in range(2):
            ld("tA", tp['A'][:, :], w_proj[c * 128:(c + 1) * 128, 0:512], w=['tA'])
            ld("tB", tp['B'][:, :], w_proj[c * 128:(c + 1) * 128, 512:1024], w=['tB'])
            act(wproj[:, c, 0:512], tp['A'][:, :], AF.Copy, r=['tA'], w=['wproj'])
            S.op('dve', lambda e, c=c: e.tensor_copy(out=wproj[:, c, 512:1024], in_=tp['B'][:, :]), r=['tB'], w=['wproj'])

        def score_fma(h, bank, dst, dkey, n):
            rl = tp['A'] if h % 2 == 0 else tp['B']
            rk = 'tA' if h % 2 == 0 else 'tB'
            act(rl[:, 0:n], psf[:, bank, 0:n], AF.Relu, r=[('f', bank)], w=[rk])
            if h == 0:
                ts('dve', dst, rl[:, 0:n], sgn[:, 0:1], None, ALU.mult, None, r=[rk, 'sgn'], w=[dkey])
            else:
                stt(dst, rl[:, 0:n], sgn[:, h:h + 1], dst, ALU.mult, ALU.add, r=[rk, 'sgn', dkey], w=[dkey])

        def bisect(L, mask_ap, mask_key, mcol0, mn_):
            allk = [('sc', c) for c in range((L + 511) // 512)]
            S.op('dve', lambda e: e.tensor_reduce(out=bis[:, 0:1], in_=scores[:, 0:L], axis=AX.X, op=ALU.min), r=allk, w=['bis0'])
            tt('dve', scores[:, mcol0:mcol0 + mn_], scores[:, mcol0:mcol0 + mn_], mask_ap, ALU.add, r=allk + [mask_key], w=allk)
            S.op('dve', lambda e: e.tensor_reduce(out=bis[:, 1:2], in_=scores[:, 0:L], axis=AX.X, op=ALU.max), r=allk, w=['bis1'])
            tt('dve', bis[:, 6:7], bis[:, 1:2], bis[:, 0:1], ALU.subtract, r=['bis0', 'bis1'], w=['bis6'])
            ts('dve', steps[:, :], pow2[:, :], bis[:, 6:7], None, ALU.mult, None, r=['pow2', 'bis6'], w=['steps'])
            tt('dve', bis[:, 2:3], bis[:, 0:1], steps[:, 0:1], ALU.add, r=['bis0', 'steps'], w=['bis2'])
            npieces = (L + 1023) // 1024
            pieces = [(pc * 1024, min(L, pc * 1024 + 1024)) for pc in range(npieces)]
            dpieces = [p_ for k_, p_ in enumerate(pieces) if k_ % 2 == 0]
            apieces = [p_ for k_, p_ in enumerate(pieces) if k_ % 2 == 1]
            nA = sum(b_ - a_ for a_, b_ in apieces)
            for rnd in range(NR):
                if apieces:
                    ts('dve', bis[:, 7:8], bis[:, 2:3], -1.0, None, ALU.mult, None, r=['bis2'], w=['bis7'])
                    for k_, (a_, b_) in enumerate(apieces):
                        act(hb[:, 0:b_ - a_], scores[:, a_:b_], AF.Sign, r=allk + ['bis7'], w=['hb', ('cntA', k_)],
                            bias=bis[:, 7:8], scale=1.0, accum_out=sm[:, 32 + k_:33 + k_])
                for k_, (a_, b_) in enumerate(dpieces):
                    if k_ == 0:
                        S.op('dve', lambda e, a_=a_, b_=b_: e.tensor_scalar(out=junk[:, 0:b_ - a_], in0=scores[:, a_:b_], scalar1=bis[:, 2:3],
                                                                         scalar2=None, op0=ALU.is_ge, op1=ALU.add, accum_out=bis[:, 3:4]),
                             r=allk + ['bis2'], w=['junk', 'bis3'])
                    else:
                        S.op('dve', lambda e, a_=a_, b_=b_: e.tensor_scalar(out=junk[:, 0:b_ - a_], in0=scores[:, a_:b_], scalar1=bis[:, 2:3],
                                                                         scalar2=bis[:, 3:4], op0=ALU.is_ge, op1=ALU.add, accum_out=bis[:, 3:4]),
                             r=allk + ['bis2', 'bis3'], w=['junk', 'bis3'])
                if apieces:
                    nap = len(apieces)
                    ck_ = [('cntA', k_) for k_ in range(nap)]
                    if nap > 1:
                        S.op('dve', lambda e, nap=nap: e.tensor_reduce(out=sm[:, 31:32], in_=sm[:, 32:32 + nap], axis=AX.X, op=ALU.add), r=ck_, w=['sm31'])
                        stt(bis[:, 3:4], sm[:, 31:32], 0.5, bis[:, 3:4], ALU.mult, ALU.add, r=['sm31', 'bis3'], w=['bis3'])
                    else:
                        stt(bis[:, 3:4], sm[:, 32:33], 0.5, bis[:, 3:4], ALU.mult, ALU.add, r=ck_ + ['bis3'], w=['bis3'])
                ts('dve', bis[:, 4:5], bis[:, 3:4], TOPK - 0.5 - 0.5 * nA, steps[:, rnd:rnd + 1], ALU.is_ge, ALU.mult, r=['bis3', 'steps'], w=['bis4'])
                stt(bis[:, 2:3], bis[:, 4:5], steps[:, rnd + 1:rnd + 2], bis[:, 2:3], ALU.subtract, ALU.add, r=['bis4', 'steps', 'bis2'], w=['bis2'])
            stt(bis[:, 5:6], steps[:, NR:NR + 1], -1.5, bis[:, 2:3], ALU.mult, ALU.add, r=['bis2', 'steps'], w=['bis5'])

        def finish_common(xtile, xkey, y_dst):
            for c in range(8):
                tr(psb[:, 0, c * 128:(c + 1) * 128], mix[:, c * 128:(c + 1) * 128], r=['mixa', 'mixg'], w=['b0'])
            act(hT[:].rearrange("p c t -> p (c t)"), psb[:, 0, :], AF.Copy, r=['b0'], w=['hT'])
            for hf in range(2):
                proj(hf, hf * 512, 512, wt=wout, wkey='wout')
            r_ = xt[1]
            for hf in range(2):
                tt('dve', r_[:, hf * 512:(hf + 1) * 512], psf[:, hf, :], xtile[:, hf * 512:(hf + 1) * 512], ALU.add,
                   r=[('f', hf), xkey], w=[('xt', 1)])
            norm_hT(r_[:, :], ('xt', 1))
            for hf in range(2):
                proj(2 + hf, hf * 512, 512, wt=wgate, wkey='wgate')
            for hf in range(2):
                act(gate[:, hf * 512:(hf + 1) * 512], psf[:, 2 + hf, :], AF.Sigmoid, r=[('f', 2 + hf)], w=['gate'])
            act(pb16[:, :], ptile[:, :], AF.Copy, r=['ptile'], w=['pb16'])
            for c in range(2):
                tr(psb[:, 1, c * 128:(c + 1) * 128], pb16[:, c * 128:(c + 1) * 128], r=['pb16'], w=[('b1', 0), ('b1', 1)])
            act(pT[:].rearrange("p c t -> p (c t)"), psb[:, 1, 0:256], AF.Copy, r=[('b1', 0), ('b1', 1)], w=['pT'])
            for hf in range(2):
                for c in range(2):
                    mm(psf[:, 4 + hf, :], pT[:, c, :], wproj[:, c, hf * 512:(hf + 1) * 512], c == 0, c == 1,
                       r=['pT', 'wproj'], w=[('f', 4 + hf)])
            for hf in range(2):
                tt('dve', gate[:, hf * 512:(hf + 1) * 512], gate[:, hf * 512:(hf + 1) * 512], psf[:, 4 + hf, :], ALU.mult,
                   r=['gate', ('f', 4 + hf)], w=['gate'])
            tt('pool', gate[:, :], gate[:, :], r_[:, :], ALU.add, r=['gate', ('xt', 1)], w=['gate'])
            st('gate', y_dst, gate[:, :], r=['gate'])

        for i in range(NOWN):
            nkb = 4 * i + 4
            L = nkb * 128
            nch = i + 1
            ld('QTz', QTz[:, :], Q_s[i], r=[('Qs', i)], w=['QTz'])
            ld('qiTz', qiTz[:, :], QI_s[i], r=[('QIs', i)], w=['qiTz'])
            ld('sgn', sgn[:, :], SG_s[i], r=[('SGs', i)], w=['sgn'])
            ld('tF', tp['F'][:, :], GA_s[i], r=[('GAs', i)], w=['tF'])
            ld('mixg', mix[:, 512:1024], GM_s[i], r=[('GMs', i)], w=['mixg'])
            ld(('xt', 0), xt[0][:, :], xown[i * 128:(i + 1) * 128, :], w=[('xt', 0)])
            ld('ptile', ptile[:, :], pown[i * 128:(i + 1) * 128, :], w=['ptile'])
            qv = qiTz[:, :].rearrange("p (h q) -> p h q", h=8)
            for c in range(nch):
                for h in range(8):
                    bank = (c * 8 + h) % 4
                    mm(psf[:, bank, :], qv[:, h, :], kiT[:, c * 512:(c + 1) * 512], True, True,
                       r=['qiTz', ('kiT', c)], w=[('f', bank)])
                    score_fma(h, bank, scores[:, c * 512:(c + 1) * 512], ('sc', c), 512)
            bisect(L, cmask[:, :], 'cmask', L - 512, 512)
            Qv = QTz[:, :].rearrange("p (g n) -> p g n", g=4)

            def ldchunk(c4):
                sl = c4 % 2
                ld(('KTc', sl), KTc[sl][:, :, :], KT_s[:, :, c4 * 512:(c4 + 1) * 512], r=[('KTs', c4)], w=[('KTc', sl)])
                ld(('Vc', sl), Vc[sl][:, :, :], V_s[:, c4 * 4:(c4 + 1) * 4, :], r=[('Vs', c4)], w=[('Vc', sl)])

            def att_A(kb):
                c4 = kb // 4
                u = kb % 4
                sl = c4 % 2
                p2 = kb % 2
                ts('dve', mk[p2][:, :], scores[:, kb * 128:(kb + 1) * 128], bis[:, 5:6], None, ALU.is_ge, None,
                   r=[('sc', c4), 'bis5'], w=[('mk', p2)])
                tr(psb[:, 1, p2 * 128:(p2 + 1) * 128], mk[p2][:, :], r=[('mk', p2)], w=[('b1', p2)])
                for g in range(4):
                    bank = 2 * p2 + g // 2
                    mm(psf[:, bank, (g % 2) * 256:(g % 2) * 256 + 256], KTc[sl][:, g // 2, u * 128:(u + 1) * 128], Qv[:, g, :], True, True,
                       r=[('KTc', sl), 'QTz'], w=[('f', bank)])
                for hf in range(2):
                    act(PT[p2][:, hf * 512:(hf + 1) * 512], psf[:, 2 * p2 + hf, :], AF.Exp, r=[('f', 2 * p2 + hf)], w=[('PT', p2)], scale=0.125)
                tt('dve', PT[p2][:, 0:1024].rearrange("p (h q) -> p h q", h=8), PT[p2][:, 0:1024].rearrange("p (h q) -> p h q", h=8),
                   bc(psb[:, 1, p2 * 128:(p2 + 1) * 128], 1, [128, 8, 128]), ALU.mult, r=[('PT', p2), ('b1', p2)], w=[('PT', p2)])

            def att_B(kb):
                c4 = kb // 4
                u = kb % 4
                sl = c4 % 2
                p2 = kb % 2
                for h in range(8):
                    bank = 4 + h // 4
                    mm(psf[:, bank, (h % 4) * 65:(h % 4) * 65 + 65], PT[p2][:, h * 128:(h + 1) * 128],
                       Vc[sl][:, u, (h // 2) * 65:(h // 2) * 65 + 65], kb == 0 and h % 4 == 0, kb == nkb - 1 and h % 4 == 3,
                       r=[('PT', p2), ('Vc', sl)], w=[('f', bank)])
                if u == 3 and c4 + 2 < nch:
                    ldchunk(c4 + 2)

            ldchunk(0)
            if nch > 1:
                ldchunk(1)
            att_A(0)
            for kb in range(1, nkb):
                att_A(kb)
                att_B(kb - 1)
            att_B(nkb - 1)
            for hf in range(2):
                o3 = psf[:, 4 + hf, 0:260].rearrange("p (h d) -> p h d", d=65)
                S.op('dve', lambda e, o3=o3, hf=hf: e.reciprocal(out=sm[:, 48 + 4 * hf:52 + 4 * hf].unsqueeze(2), in_=o3[:, :, 64:65]),
                     r=[('f', 4 + hf)], w=['sm48'])
                tt('dve', tp['C'][:, hf * 256:(hf + 1) * 256].rearrange("p (h d) -> p h d", d=64), o3[:, :, 0:64],
                   bc(sm[:, 48 + 4 * hf:52 + 4 * hf], 2, [128, 4, 64]), ALU.mult, r=[('f', 4 + hf), 'sm48'], w=['tC'])
            tt('dve', mix[:, 0:512], tp['C'][:, :], tp['F'][:, :], ALU.mult, r=['tC', 'tF'], w=['mixa'])
            finish_common(xt[0], ('xt', 0), y_own[i * 128:(i + 1) * 128, :])

        S.barrier()

        for c in range(8):
            for (a, b_) in ((0, 1024), (1024, 2048), (2048, 3072), (3072, DIN)):
                n_ = b_ - a
                ld("gate", gate[:, 0:n_], w_in[c * 128:(c + 1) * 128, a:b_], w=['gate'])
                if a == 0:
                    for G_ in range(2):
                        ts('dve', win[:, c, G_ * 256:(G_ + 1) * 256].rearrange("p (r e d) -> p r e d", r=2, e=2),
                           gate[:, G_ * 256:(G_ + 1) * 256].rearrange("p (e r d) -> p r e d", e=2, r=2),
                           gcol[:, c:c + 1], None, ALU.mult, None, r=['gate', 'gcol'], w=['win'])
                    act(win[:, c, 512:1024], gate[:, 512:1024], AF.Copy, r=['gate', 'gcol'], w=['win'], scale=gcol[:, c:c + 1])
                else:
                    act(win[:, c, a:b_], gate[:, 0:n_], AF.Copy, r=['gate', 'gcol'], w=['win'], scale=gcol[:, c:c + 1])
        ld(('xt', 0), xt[0][:, :], xs, w=[('xt', 0)])
        ld('ptile', ptile[:, :], ps_d, w=['ptile'])
        norm_hT(xt[0][:, :], ('xt', 0))
        full_proj()
        kv_post(cosS[:, :], sinS[:, :], 'cosS', ks, vs, kis, KTn[:, :, :], 'KTn', Vn[:, :], 'Vn', kiTn[:, :], 'kiTn')
        q_post(cosS[:, :], sinS[:, :], 'cosS', wsTS, 'wsTS', bsS, 'bsS', vns, mix[:, 512:1024], 'mixg')
        S.barrier()

        for i in range(2):
            S.op('pool', lambda e, i=i: e.memset(Vp[i], 1.0), w=[('Vp', i)])
        ld(('xt', 1), xt[1][:, :].bitcast(I32), pt_d.rearrange("(o s) n -> o (s n)", o=1).partition_broadcast(128), w=[('xt', 1)])
        ts('dve', gate[:, :].bitcast(I32), xt[1][:, :].bitcast(I32), 128.0, piof[:, 0:1], ALU.mult, ALU.add,
           r=[('xt', 1), 'piof'], w=['gate'])
        S.op('pool', lambda e: e.memset(qiTzS, 0.0), w=['qiTzS'])
        qv = qiTz[:, :].rearrange("p (h q) -> p h q", h=8)
        for s_ in range(16):
            S.op('dve', lambda e, s_=s_: e.tensor_copy(out=qiTzS[:, s_, :, 8 * s_:8 * s_ + 8], in_=qv[:, :, 8 * s_:8 * s_ + 8]),
                 r=['qiTz', 'qiTzS'], w=['qiTzS'])
        for c in range(16):
            half = c % 2
            for s_ in range(16):
                sl = s_ % 2
                for u in range(4):
                    S.dma('pool', ('kraw', sl), lambda e, s_=s_, u=u, sl=sl, c=c: e.indirect_dma_start(
                        out=kraw[sl][:, u, :], out_offset=None, in_=cik,
                        in_offset=bass.IndirectOffsetOnAxis(ap=idxall[:, s_, 4 * c + u:4 * c + u + 1], axis=0)),
                        r=['gate'], w=[('kraw', sl)])
                S.op('dve', lambda e, sl=sl: e.tensor_copy(out=kidup[:, :, :].rearrange("p u (a d) -> p u a d", a=2),
                                                          in_=bc(kraw[sl], 2, [128, 4, 2, 64])),
                     r=[('kraw', sl)], w=['kidup'])
                bkeys = ['b0'] if sl == 0 else [('b1', 0), ('b1', 1), 'b1r']
                for u in range(4):
                    tr(psb[:, sl, u * 128:(u + 1) * 128], kidup[:, u, :], r=['kidup'], w=bkeys)
                act(kiTc[:, half * 16 + s_, :], psb[:, sl, 0:512], AF.Copy, r=bkeys, w=[('kiTc', half)])
            for h in range(8):
                bank = h % 4
                for s_ in range(16):
                    mm(psf[:, bank, :], qiTzS[:, s_, h, :], kiTc[:, half * 16 + s_, :], s_ == 0, s_ == 15,
                       r=['qiTzS', ('kiTc', half)], w=[('f', bank)])
                score_fma(h, bank, scores[:, c * 512:(c + 1) * 512], ('sc', c), 512)
        for h in range(8):
            bank = h % 4
            mm(psf[:, bank, 0:128], qv[:, h, :], kiTn[:, :], True, True, r=['qiTz', 'kiTn'], w=[('f', bank)])
            score_fma(h, bank, scores[:, PAST:PAST + 128], ('sc', 16), 128)
        bisect(LS, newneg[:, :], 'newneg', PAST, 128)
        Qv5 = QTz[:, :].rearrange("p (g r q) -> p g r q", g=4, r=2)
        def satt_A(s_, pg):
            it = s_ * (NPG + 1) + pg
            sl = it % 2
            if pg < NPG:
                S.dma('pool', ('kraw2', sl), lambda e: e.indirect_dma_start(
                    out=kraw2[sl], out_offset=None, in_=ck,
                    in_offset=bass.IndirectOffsetOnAxis(ap=idxall[:, s_, pg:pg + 1], axis=0)), r=['gate'], w=[('kraw2', sl)])
                S.dma('pool', ('vraw', sl), lambda e: e.indirect_dma_start(
                    out=vraw[sl], out_offset=None, in_=cv,
                    in_offset=bass.IndirectOffsetOnAxis(ap=idxall[:, s_, pg:pg + 1], axis=0)), r=['gate'], w=[('vraw', sl)])
                act(kb16[:, :], kraw2[sl], AF.Copy, r=[('kraw2', sl)], w=['kb16'])
                for kc in range(2):
                    tr(psb[:, 0, kc * 128:(kc + 1) * 128], kb16[:, kc * 128:(kc + 1) * 128], r=['kb16'], w=['b0'])
                S.op('dve', lambda e: e.tensor_copy(out=KTp[sl], in_=psb[:, 0, 0:256].rearrange("p (c t) -> p c t", c=2)),
                     r=['b0'], w=[('KTp', sl)])
                act(Vp[sl].rearrange("p (g d) -> p g d", d=65)[:, :, 0:64], vraw[sl].rearrange("p (g d) -> p g d", d=64),
                    AF.Copy, r=[('vraw', sl)], w=[('Vp', sl)])
                KT_ = KTp[sl]
                ktk = ('KTp', sl)
            else:
                KT_ = KTn
                ktk = 'KTn'
            ts('dve', mk[sl][:, :], scores[:, pg * 128:(pg + 1) * 128], bis[:, 5:6], None, ALU.is_ge, None,
               r=[('sc', pg // 4), 'bis5'], w=[('mk', sl)])
            tr(psb[:, 1, sl * 128:(sl + 1) * 128], mk[sl][:, :], r=[('mk', sl)], w=[('b1', sl)])
            for g in range(4):
                mm(psf[:, sl, g * 16:(g + 1) * 16], KT_[:, g // 2, :], Qv5[:, g, :, 8 * s_:8 * s_ + 8], True, True,
                   r=[ktk, 'QTz'], w=[('f', sl)])
            act(PTs[sl][:, :], psf[:, sl, 0:64], AF.Exp, r=[('f', sl)], w=[('PTs', sl)], scale=0.125)
            tt('dve', PTs[sl][:, :].rearrange("p (h t) -> p h t", h=8), PTs[sl][:, :].rearrange("p (h t) -> p h t", h=8),
               bc(psb[:, 1, sl * 128 + 8 * s_:sl * 128 + 8 * s_ + 8], 1, [128, 8, 8]), ALU.mult,
               r=[('PTs', sl), ('b1', sl)], w=[('PTs', sl)])

        def satt_B(s_, pg):
            it = s_ * (NPG + 1) + pg
            sl = it % 2
            if pg < NPG:
                V_ = Vp[sl]
                vk = ('Vp', sl)
            else:
                V_ = Vn
                vk = 'Vn'
            for g in range(4):
                mm(psf[0:16, 4, g * 65:(g + 1) * 65], PTs[sl][:, g * 16:(g + 1) * 16], V_[:, g * 65:(g + 1) * 65],
                   pg == 0 and g == 0, pg == NPG and g == 3, r=[('PTs', sl), vk], w=[('f', 4)])
            if pg == NPG:
                o3 = psf[0:16, 4, 0:260].rearrange("p (g d) -> p g d", d=65)
                S.op('dve', lambda e: e.reciprocal(out=sm[0:16, 56:60].unsqueeze(2), in_=o3[:, :, 64:65]), r=[('f', 4)], w=['sm56'])
                tt('dve', On.rearrange("p (g d) -> p g d", d=64), o3[:, :, 0:64], bc(sm[0:16, 56:60], 2, [16, 4, 64]), ALU.mult,
                   r=[('f', 4), 'sm56'], w=['On'])
                On2v = On2.rearrange("p (g r d) -> p g r d", g=4, r=2)
                for r_i in range(2):
                    ts('dve', On2v[:, :, r_i, :], On.rearrange("p (g d) -> p g d", d=64), m01[:, r_i:r_i + 1], None, ALU.mult, None,
                       r=['On', 'm01'], w=['On2'])
                mm(psf[:, 5, :], selb[:, 120 - 8 * s_:248 - 8 * s_], On2, s_ == 0, s_ == 15, r=['selb', 'On2'], w=[('f', 5)])

        its = [(s_, pg) for s_ in range(16) for pg in range(NPG + 1)]
        satt_A(*its[0])
        for k_ in range(1, len(its)):
            satt_A(*its[k_])
            satt_B(*its[k_ - 1])
        satt_B(*its[-1])
        tt('dve', mix[:, 0:512], psf[:, 5, :], tp['F'][:, :], ALU.mult, r=[('f', 5), 'tF'], w=['mixa'])
        finish_common(xt[0], ('xt', 0), ys)

        S.barrier()
    return nc


_NC_CACHE = {}


def _consts(j):
    c = {}
    c["identb"] = np.eye(128, dtype=np.float32).astype(ml_dtypes.bfloat16)
    ii = np.arange(128)[:, None]
    jj = np.arange(128)[None, :]
    c["tril"] = (jj <= ii).astype(np.float32)
    tri_neg = np.where(jj <= ii, 0.0, NEG).astype(np.float32)
    cm = np.zeros((128, 4, 128), np.float32)
    for u in range(4):
        if u == j:
            cm[:, u, :] = tri_neg
        elif u > j:
            cm[:, u, :] = NEG
    c["cmask"] = cm.reshape(128, 512)
    same = (ii // 8) == (jj // 8)
    c["newneg"] = np.where(same & ((jj % 8) <= (ii % 8)), 0.0, NEG).astype(np.float32)
    half = 8
    inv = np.power(np.float32(500000.0), -np.arange(half, dtype=np.float32) * np.float32(2.0) / np.float32(16))

    def cs(pos):
        ang = pos.astype(np.float32)[..., None] * inv
        return np.cos(ang).astype(np.float32), np.sin(ang).astype(np.float32)
    posK = (np.arange(NBLK)[None, :] * 128 + np.arange(128)[:, None])
    c["cosK"], c["sinK"] = cs(posK)
    posQ = ((np.arange(NOWN)[None, :] * 4 + j) * 128 + np.arange(128)[:, None])
    c["cosQ"], c["sinQ"] = cs(posQ)
    posS = PAST + (np.arange(128) % 8)
    c["cosS"], c["sinS"] = cs(posS)
    selb = np.zeros((16, 248), np.float32)
    for r in range(2):
        for t in range(8):
            selb[r * 8 + t, 120 + t] = 1.0
    c["selb"] = selb
    m01 = np.zeros((16, 2), np.float32)
    m01[0:8, 0] = 1.0
    m01[8:16, 1] = 1.0
    c["m01"] = m01
    c["pow2"] = np.tile((2.0 ** -(np.arange(NR + 2) + 1.0)).astype(np.float32)[None, :], (128, 1))
    c["piof"] = np.arange(128, dtype=np.float32)[:, None]
    return c


def kernel(x_prompt, x_sample, cache_k, cache_v, cache_idx_k, page_table, p_prompt, p_sample,
           norm_in_g, w_in, q_norm_g, k_norm_g, ln_v_g, ln_v_b, w_s, b_s, w_out,
           ple_norm_g, w_ple_gate, w_ple_proj):
    f = lambda a: np.ascontiguousarray(np.asarray(a))
    x_prompt, x_sample, p_prompt, p_sample = f(x_prompt), f(x_sample), f(p_prompt), f(p_sample)
    ck = f(cache_k).reshape(NPOOL * 128, 256)
    cvv = f(cache_v).reshape(NPOOL * 128, 256)
    cik = f(cache_idx_k).reshape(NPOOL * 128, 64)
    page_table = f(page_table).astype(np.int32)
    w_s = f(w_s)[0]
    b_s = f(b_s)[0]
    if "nc" not in _NC_CACHE:
        _NC_CACHE["nc"] = build()
    nc = _NC_CACHE["nc"]
    wsS = np.zeros((128, 8, 128), np.float32)
    for s_ in range(16):
        wsS[8 * s_:8 * s_ + 8, :, 8 * s_:8 * s_ + 8] = np.transpose(w_s[:, 0:8, 0:8], (1, 0, 2))
    shared = {
        "ck": ck, "cv": cvv, "cik": cik,
        "w_in": f(w_in)[0], "w_out": f(w_out)[0], "w_gate": f(w_ple_gate)[0], "w_proj": f(w_ple_proj)[0],
        "gin_col": f(f(norm_in_g)[0].reshape(8, 128).T), "gple_col": f(f(ple_norm_g)[0].reshape(8, 128).T),
        "gq": f(q_norm_g).reshape(1, 64), "gk": f(k_norm_g).reshape(1, 64),
        "lng": f(ln_v_g).reshape(1, 512), "lnb": f(ln_v_b).reshape(1, 512),
        "ws": f(np.transpose(w_s, (1, 0, 2))), "wsS": wsS,
        "bsT": f(b_s.T), "bsS": f(np.tile(b_s[:, 0:8].T, (16, 1))),
    }
    in_maps = []
    for c in range(8):
        b, j = c // 4, c % 4
        m = dict(shared)
        m.update(_consts(j))
        m["xb"] = x_prompt[b]
        m["xown"] = f(x_prompt[b].reshape(NBLK, 128, D)[j::4].reshape(NOWN * 128, D))
        m["pown"] = f(p_prompt[0, b].reshape(NBLK, 128, 256)[j::4].reshape(NOWN * 128, 256))
        m["xs"] = f(x_sample[16 * c:16 * c + 16].reshape(128, D))
        m["ps"] = f(p_sample[0, 16 * c:16 * c + 16].reshape(128, 256))
        m["pt"] = f(page_table[16 * c:16 * c + 16])
        in_maps.append(m)
    res = run_bass_kernel_spmd(nc, in_maps, core_ids=list(range(8))).results

    def own(name, width):
        out = np.zeros((2, NBLK, 128, width), np.float32)
        for c in range(8):
            b, j = c // 4, c % 4
            out[b, j::4] = res[c][name].reshape(NOWN, 128, width)
        return out.reshape(2, SEQ, width)
    y_prompt = own("y_own", D)
    k_prompt = own("k_own", 256).reshape(1, 2, SEQ, 4, 64)
    v_prompt = own("v_own", 256).reshape(1, 2, SEQ, 4, 64)
    ik_prompt = own("ki_own", 64).reshape(1, 2, SEQ, 64)
    cat = lambda name, shp: np.concatenate([res[c][name] for c in range(8)], axis=0).reshape(shp)
    y_sample = cat("ys", (128, 8, D))
    k_sample = cat("ks", (1, 128, 8, 4, 64))
    v_sample = cat("vs", (1, 128, 8, 4, 64))
    ik_sample = cat("kis", (1, 128, 8, 64))
    vn_sample = cat("vns", (1, 128, 8, 512))
    return (y_prompt, y_sample, k_prompt, v_prompt, ik_prompt, k_sample, v_sample, ik_sample, vn_sample)
```

```python
import contextlib
import numpy as np
import ml_dtypes
import concourse.bass as bass
import concourse.mybir as mybir
from concourse.bass_utils import run_bass_kernel_spmd

F32 = mybir.dt.float32
BF16 = mybir.dt.bfloat16
I32 = mybir.dt.int32
ALU = mybir.AluOpType
AF = mybir.ActivationFunctionType
AX = mybir.AxisListType

D = 1024
DIN = 3656
SEQ = 16384
NBLK = 128
NOWN = 32
NPOOL = 10240
NPG = 64
PAST = 8192
NEG = -1.0e30
EPS = 1e-6
NR = 18
TOPK = 256
LS = PAST + 128


class Sched:
    def __init__(self, nc, es):
        self.nc = nc
        self.es = es
        self.eng = {'pe': nc.tensor, 'act': nc.scalar, 'dve': nc.vector, 'pool': nc.gpsimd, 'sp': nc.sync}
        self.sem = {}
        self.cnt = {}
        self.seen = {e: {} for e in self.eng}
        self.last_w = {}
        self.readers = {}
        for e in ('pe', 'act', 'dve', 'pool'):
            self._mksem(e)

    def _mksem(self, k):
        self.sem[k] = self.es.enter_context(self.nc.semaphore("s_%d" % len(self.sem)))
        self.cnt[k] = 0

    def _deps(self, e, r, w):
        need = {}

        def add(ev):
            if ev is not None:
                need[ev[0]] = max(need.get(ev[0], 0), ev[1])
        for b in r:
            add(self.last_w.get(b))
        for b in w:
            add(self.last_w.get(b))
            for ev in self.readers.get(b, ()):
                add(ev)
        for k, v in need.items():
            if e == 'pe' and k == 'pe':
                continue
            if self.seen[e].get(k, 0) < v:
                self.eng[e].wait_ge(self.sem[k], v)
                self.seen[e][k] = v

    def _record(self, ev, r, w):
        for b in r:
            lst = self.readers.setdefault(b, [])
            lst[:] = [x for x in lst if x[0] != ev[0]]
            lst.append(ev)
        for b in w:
            self.last_w[b] = ev
            self.readers[b] = []

    def op(self, e, fn, r=(), w=()):
        self._deps(e, r, w)
        ins = fn(self.eng[e])
        self.cnt[e] += 1
        ins.then_inc(self.sem[e], 1)
        self._record((e, self.cnt[e]), r, w)
        return ins

    def dma(self, q, semkey, fn, r=(), w=()):
        k = ('d', semkey)
        if k not in self.sem:
            self._mksem(k)
        self._deps(q, r, w)
        ins = fn(self.eng[q])
        self.cnt[k] += 16
        ins.then_inc(self.sem[k], 16)
        self._record((k, self.cnt[k]), r, w)
        return ins

    def wait_all(self, e='sp'):
        for k, v in self.cnt.items():
            if v > 0 and self.seen[e].get(k, 0) < v:
                self.eng[e].wait_ge(self.sem[k], v)
                self.seen[e][k] = v

    def barrier(self):
        for e in self.eng:
            self.wait_all(e)


def build():
    nc = bass.Bass("TRN2", target_bir_lowering=False)

    def din(name, shape, dt=F32):
        return nc.dram_tensor(name, list(shape), dt, kind="ExternalInput").ap()

    def dout(name, shape, dt=F32):
        return nc.dram_tensor(name, list(shape), dt, kind="ExternalOutput").ap()

    def dscr(name, shape, dt):
        return nc.dram_tensor(name, list(shape), dt, kind="Internal").ap()

    xb = din("xb", [SEQ, D])
    xown = din("xown", [NOWN * 128, D])
    pown = din("pown", [NOWN * 128, 256])
    xs = din("xs", [128, D])
    ps_d = din("ps", [128, 256])
    ck = din("ck", [NPOOL * 32, 1024])
    cv = din("cv", [NPOOL * 32, 1024])
    cik = din("cik", [NPOOL * 32, 256])
    pt_d = din("pt", [16, NPG], I32)
    w_in = din("w_in", [D, DIN])
    w_out = din("w_out", [D, D])
    w_gate = din("w_gate", [D, D])
    w_proj = din("w_proj", [256, D])
    gin_col = din("gin_col", [128, 8])
    gple_col = din("gple_col", [128, 8])
    gq_d = din("gq", [1, 64])
    gk_d = din("gk", [1, 64])
    lng_d = din("lng", [1, 512])
    lnb_d = din("lnb", [1, 512])
    ws_d = din("ws", [128, 8, 128])
    wsS_d = din("wsS", [128, 8, 128])
    bsT_d = din("bsT", [128, 8])
    bsS_d = din("bsS", [128, 8])
    identb_d = din("identb", [128, 128], BF16)
    tril_d = din("tril", [128, 128])
    cmask_d = din("cmask", [128, 512])
    newneg_d = din("newneg", [128, 128])
    cosK_d = din("cosK", [128, NBLK, 8])
    sinK_d = din("sinK", [128, NBLK, 8])
    cosQ_d = din("cosQ", [128, NOWN, 8])
    sinQ_d = din("sinQ", [128, NOWN, 8])
    cosS_d = din("cosS", [128, 8])
    sinS_d = din("sinS", [128, 8])
    selb_d = din("selb", [16, 248])
    m01_d = din("m01", [16, 2])
    pow2_d = din("pow2", [128, NR + 2])
    piof_d = din("piof", [128, 1])

    y_own = dout("y_own", [NOWN * 128, D])
    k_own = dout("k_own", [NOWN * 128, 256])
    v_own = dout("v_own", [NOWN * 128, 256])
    ki_own = dout("ki_own", [NOWN * 128, 64])
    ys = dout("ys", [128, D])
    ks = dout("ks", [128, 256])
    vs = dout("vs", [128, 256])
    kis = dout("kis", [128, 64])
    vns = dout("vns", [128, 512])

    KT_s = dscr("KT_s", [128, 2, SEQ], BF16)
    V_s = dscr("V_s", [128, NBLK, 260], BF16)
    Q_s = dscr("Q_s", [NOWN, 128, 1024], BF16)
    QI_s = dscr("QI_s", [NOWN, 128, 1024], BF16)
    SG_s = dscr("SG_s", [NOWN, 128, 8], F32)
    GA_s = dscr("GA_s", [NOWN, 128, 512], F32)
    GM_s = dscr("GM_s", [NOWN, 128, 512], BF16)

    with contextlib.ExitStack() as es:
        S = Sched(nc, es)

        def sb(name, shape, dt=F32):
            return es.enter_context(nc.sbuf_tensor("sb_" + name, list(shape), dt))

        big = sb("big", [128, 16640], F32)
        kiT = sb("kiT", [128, SEQ], BF16)
        R2 = sb("R2", [128, 9216], F32)
        psf = es.enter_context(nc.psum_tensor("psf", [128, 6, 512], F32))
        psb = es.enter_context(nc.psum_tensor("psb", [128, 2, 1024], BF16))

        win = big[:].bitcast(BF16)[:, 0:8 * DIN].rearrange("p (c n) -> p c n", c=8)
        scores = big
        qiTzS = big[:].bitcast(BF16)[:, 2 * LS:2 * LS + 16384].rearrange("p (s h q) -> p s h q", s=16, h=8)
        kiTc = kiT[:, :].rearrange("p (s n) -> p s n", s=32)
        stage = R2[:, 0:DIN]
        cosK = R2[:, 3656:3656 + 1024].rearrange("p (b e) -> p b e", e=8)
        sinK = R2[:, 4680:4680 + 1024].rearrange("p (b e) -> p b e", e=8)
        cosQ = R2[:, 5704:5704 + 256].rearrange("p (b e) -> p b e", e=8)
        sinQ = R2[:, 5960:5960 + 256].rearrange("p (b e) -> p b e", e=8)
        R2b = R2[:].bitcast(BF16)
        wout = R2b[:, 0:8192].rearrange("p (c n) -> p c n", c=8)
        wgate = R2b[:, 8192:16384].rearrange("p (c n) -> p c n", c=8)
        wproj = R2b[:, 16384:18432].rearrange("p (c n) -> p c n", c=2)

        xt = [sb("xt%d" % i, [128, D]) for i in range(2)]
        gate = sb("gate", [128, D])
        hb = sb("hb", [128, D], BF16)
        hT = sb("hT", [128, 8, 128], BF16)
        mix = sb("mix", [128, D], BF16)
        tp = {n: sb("t_" + n, [128, 512]) for n in "ABCDFGH"}
        tp['E'] = tp['D']
        qb = sb("qb", [128, 512], BF16)
        qib = sb("qib", [128, 512], BF16)
        vnb = sb("vnb", [128, 512], BF16)
        kb16 = sb("kb16", [128, 256], BF16)
        kidup = sb("kidup", [128, 4, 128], BF16)
        QTz = sb("QTz", [128, 1024], BF16)
        qiTz = sb("qiTz", [128, 1024], BF16)
        sgn = sb("sgn", [128, 8])
        KTn = sb("KTn", [128, 2, 128], BF16)
        Vn = sb("Vn", [128, 260], BF16)
        kiTn = sb("kiTn", [128, 128], BF16)
        PT = [sb("PT%d" % i, [128, 1040], BF16) for i in range(2)]
        ktb4 = PT[0][:, 0:1024].rearrange("p (c n) -> p c n", c=2)
        vaug4 = PT[1][:, 0:1040].rearrange("p (u n) -> p u n", u=4)
        KTc = [sb("KTc%d" % i, [128, 2, 512], BF16) for i in range(2)]
        Vc = [sb("Vc%d" % i, [128, 4, 260], BF16) for i in range(2)]
        mk = [sb("mk%d" % i, [128, 128], BF16) for i in range(2)]
        junk = sb("junk", [128, 1024], BF16)
        ptile = sb("ptile", [128, 256])
        pb16 = sb("pb16", [128, 256], BF16)
        pT = sb("pT", [128, 2, 128], BF16)
        sm = sb("sm", [128, 64])
        rp = sb("rp", [128, 6, 64])
        ident = sb("ident", [128, 128], BF16)
        tril = sb("tril", [128, 128])
        cmask = sb("cmask", [128, 512])
        newneg = sb("newneg", [128, 128])
        gq = sb("gq", [128, 64])
        gk = sb("gk", [128, 64])
        lng = sb("lng", [128, 512])
        lnb = sb("lnb", [128, 512])
        bsT = sb("bsT", [128, 8])
        bsS = sb("bsS", [128, 8])
        wsT = sb("wsT", [128, 8, 128], BF16)
        wsTS = sb("wsTS", [128, 8, 128], BF16)
        cosS = sb("cosS", [128, 8])
        sinS = sb("sinS", [128, 8])
        selb = sb("selb", [16, 248])
        m01 = sb("m01", [16, 2])
        pow2 = sb("pow2", [128, NR + 2])
        piof = sb("piof", [128, 1])
        gcol = sb("gcol", [128, 16])
        steps = sb("steps", [128, NR + 2])
        bis = sb("bis", [128, 8])
        ptb = xt[1][:, :].bitcast(I32).rearrange("p (s n) -> p s n", s=16)
        idxall = gate[:, :].bitcast(I32).rearrange("p (s n) -> p s n", s=16)
        kraw = [hb[:, :].bitcast(F32)[:, i * 256:(i + 1) * 256].rearrange("p (u d) -> p u d", u=4) for i in range(2)]
        kraw2 = [tp['D'][:, i * 256:(i + 1) * 256] for i in range(2)]
        vraw = [tp['G'][:, i * 256:(i + 1) * 256] for i in range(2)]
        KTp = [KTc[i][:, :, 0:128] for i in range(2)]
        Vp = [Vc[i][:, 0, :] for i in range(2)]
        PTs = [sb("PTs%d" % i, [128, 64], BF16) for i in range(2)]
        On = tp['C'][0:16, 0:256]
        On2 = tp['H'][0:16, :]

        def act(out, in_, func, r, w, **kw):
            return S.op('act', lambda e: e.activation(out=out, in_=in_, func=func, **kw), r=r, w=w)

        def ts(eng, out, in0, s1, s2, op0, op1, r, w, accum_out=None):
            if op1 is None:
                return S.op(eng, lambda e: e.tensor_scalar(out=out, in0=in0, scalar1=s1, scalar2=None, op0=op0), r=r, w=w)
            if accum_out is not None:
                return S.op(eng, lambda e: e.tensor_scalar(out=out, in0=in0, scalar1=s1, scalar2=s2, op0=op0, op1=op1, accum_out=accum_out), r=r, w=w)
            return S.op(eng, lambda e: e.tensor_scalar(out=out, in0=in0, scalar1=s1, scalar2=s2, op0=op0, op1=op1), r=r, w=w)

        def tt(eng, out, in0, in1, op, r, w):
            return S.op(eng, lambda e: e.tensor_tensor(out=out, in0=in0, in1=in1, op=op), r=r, w=w)

        def stt(out, in0, scalar, in1, op0, op1, r, w):
            return S.op('dve', lambda e: e.scalar_tensor_tensor(out=out, in0=in0, scalar=scalar, in1=in1, op0=op0, op1=op1), r=r, w=w)

        def mm(out, lhsT, rhs, start, stop, r, w):
            return S.op('pe', lambda e: e.matmul(out, lhsT, rhs, start=start, stop=stop, skip_group_check=True), r=r, w=w)

        def tr(out, in_, r, w, idn=None):
            return S.op('pe', lambda e: e.transpose(out=out, in_=in_, identity=ident[:, :] if idn is None else idn), r=list(r) + ['ident'], w=w)

        def ld(semkey, out, in_, w, r=(), q='sp'):
            return S.dma(q, semkey, lambda e: e.dma_start(out=out, in_=in_), r=r, w=w)

        def st(semkey, out, in_, r, w=()):
            return S.dma('sp', semkey, lambda e: e.dma_start(out=out, in_=in_), r=r, w=w)

        def bc(ap, axis, shape):
            return ap.unsqueeze(axis).to_broadcast(list(shape))

        for name, t_, d_ in [("ident", ident, identb_d), ("tril", tril, tril_d), ("cmask", cmask, cmask_d),
                             ("newneg", newneg, newneg_d), ("bsT", bsT, bsT_d), ("bsS", bsS, bsS_d),
                             ("cosS", cosS, cosS_d), ("sinS", sinS, sinS_d), ("selb", selb, selb_d),
                             ("m01", m01, m01_d), ("pow2", pow2, pow2_d), ("piof", piof, piof_d)]:
            ld(name, t_[:], d_, w=[name])
        ld("gcol", gcol[:, 0:8], gin_col, w=["gcol"])
        ld("gcol", gcol[:, 8:16], gple_col, w=["gcol"])
        ld("gq", gq[:], gq_d.partition_broadcast(128), w=["gq"])
        ld("gk", gk[:], gk_d.partition_broadcast(128), w=["gk"])
        ld("lng", lng[:], lng_d.partition_broadcast(128), w=["lng"])
        ld("lnb", lnb[:], lnb_d.partition_broadcast(128), w=["lnb"])
        ld("R2", cosK, cosK_d, w=["cosK"])
        ld("R2", sinK, sinK_d, w=["cosK"])
        ld("R2", cosQ, cosQ_d, w=["cosK"])
        ld("R2", sinQ, sinQ_d, w=["cosK"])
        S.op('pool', lambda e: e.memset(QTz[:], 0.0), w=['QTz'])
        S.op('pool', lambda e: e.memset(qiTz[:], 0.0), w=['qiTz'])
        S.op('pool', lambda e: e.memset(vaug4, 1.0), w=['vaug4'])
        S.op('pool', lambda e: e.memset(Vn[:], 1.0), w=['Vn'])
        for (src_d, dst, nm) in ((ws_d, wsT, 'wsT'), (wsS_d, wsTS, 'wsTS')):
            for g in range(8):
                ld("tA", tp['A'][:, 0:128], src_d[:, g, :], w=['tA'])
                tt('dve', qb[:, 0:128], tp['A'][:, 0:128], tril[:, :], ALU.mult, r=['tA', 'tril'], w=['qb'])
                tr(psb[:, 0, 0:128], qb[:, 0:128], r=['qb'], w=['b0'])
                act(dst[:, g, :], psb[:, 0, 0:128], AF.Copy, r=['b0'], w=[nm])
        for c in range(8):
            ld("stage", stage, w_in[c * 128:(c + 1) * 128, :], w=['stage'])
            for G_ in range(2):
                ts('dve', win[:, c, G_ * 256:(G_ + 1) * 256].rearrange("p (r e d) -> p r e d", r=2, e=2),
                   stage[:, G_ * 256:(G_ + 1) * 256].rearrange("p (e r d) -> p r e d", e=2, r=2),
                   gcol[:, c:c + 1], None, ALU.mult, None, r=['stage', 'gcol'], w=['win'])
            act(win[:, c, 512:2048], stage[:, 512:2048], AF.Copy, r=['stage', 'gcol'], w=['win'], scale=gcol[:, c:c + 1])
            ts('pool', win[:, c, 2048:DIN], stage[:, 2048:DIN], gcol[:, c:c + 1], None, ALU.mult, None, r=['stage', 'gcol'], w=['win'])

        def norm_hT(xtile, xkey, wkey_suffix=''):
            act(junk[:, 0:1024], xtile, AF.Square, r=[xkey], w=['junk', 'sm0'], accum_out=sm[:, 0:1])
            act(sm[:, 1:2], sm[:, 0:1], AF.Sqrt, r=['sm0'], w=['sm1'], scale=1.0 / D, bias=epsb[:, 0:1])
            S.op('dve', lambda e: e.reciprocal(out=sm[:, 2:3], in_=sm[:, 1:2]), r=['sm1'], w=['sm2'])
            ts('dve', hb[:, :], xtile, sm[:, 2:3], None, ALU.mult, None, r=[xkey, 'sm2'], w=['hb'])
            for c in range(8):
                tr(psb[:, 0, c * 128:(c + 1) * 128], hb[:, c * 128:(c + 1) * 128], r=['hb'], w=['b0'])
            act(hT[:].rearrange("p c t -> p (c t)"), psb[:, 0, :], AF.Copy, r=['b0'], w=['hT'])

        def proj(bank, col0, n, wt=None, wkey='win'):
            wt_ = win if wt is None else wt
            for c in range(8):
                mm(psf[:, bank, 0:n], hT[:, c, :], wt_[:, c, col0:col0 + n], c == 0, c == 7,
                   r=['hT', wkey], w=[('f', bank)])

        def norm_rope(src, srckey, H, gt, gkey, cos, sin, cskey, dst, dstkey, do_norm):
            W = H * 64
            src3 = src.rearrange("p (h d) -> p h d", d=64)
            dst3 = dst.rearrange("p (h d) -> p h d", d=64)
            if do_norm:
                act(tp['A'][:, 0:W], src, AF.Square, r=[srckey], w=['tA'])
                S.op('dve', lambda e: e.tensor_reduce(out=sm[:, 8:8 + H], in_=tp['A'][:, 0:W].rearrange("p (h d) -> p h d", d=64),
                                                      axis=AX.X, op=ALU.add), r=['tA'], w=['sm8'])
                act(sm[:, 16:16 + H], sm[:, 8:8 + H], AF.Sqrt, r=['sm8'], w=['sm16'], scale=1.0 / 64, bias=epsb[:, 0:1])
                S.op('dve', lambda e: e.reciprocal(out=sm[:, 24:24 + H], in_=sm[:, 16:16 + H]), r=['sm16'], w=['sm24'])
                tt('dve', dst3, src3, bc(sm[:, 24:24 + H], 2, [128, H, 64]), ALU.mult, r=[srckey, 'sm24'], w=[dstkey])
                tt('dve', dst3, dst3, bc(gt[:, :], 1, [128, H, 64]), ALU.mult, r=[dstkey, gkey], w=[dstkey])
            else:
                act(dst, src, AF.Copy, r=[srckey], w=[dstkey])
            x1 = dst3[:, :, 0:8]
            x2 = dst3[:, :, 8:16]
            cb = bc(cos, 1, [128, H, 8])
            sb_ = bc(sin, 1, [128, H, 8])
            ra = rp[:, 0, 0:H * 8].rearrange("p (h e) -> p h e", e=8)
            rb = rp[:, 1, 0:H * 8].rearrange("p (h e) -> p h e", e=8)
            rc = rp[:, 2, 0:H * 8].rearrange("p (h e) -> p h e", e=8)
            rd = rp[:, 3, 0:H * 8].rearrange("p (h e) -> p h e", e=8)
            tt('dve', ra, x1, cb, ALU.mult, r=[dstkey, cskey], w=['rp'])
            tt('dve', rb, x2, sb_, ALU.mult, r=[dstkey, cskey], w=['rp'])
            tt('dve', rc, x2, cb, ALU.mult, r=[dstkey, cskey], w=['rp'])
            tt('dve', rd, x1, sb_, ALU.mult, r=[dstkey, cskey], w=['rp'])
            tt('dve', x1, ra, rb, ALU.subtract, r=['rp'], w=[dstkey])
            tt('dve', x2, rc, rd, ALU.add, r=['rp'], w=[dstkey])

        epsb = sb("epsb", [128, 1])
        S.op('pool', lambda e: e.memset(epsb[:], EPS), w=['epsb'])

        def kv_post(cos, sin, cskey, k_out, v_out, ki_out, kt_dst, kt_key, v_dst, v_key, kiT_dst, kiT_key, bkv=0, bki=1):
            norm_rope(psf[:, bkv, 0:256], ('f', bkv), 4, gk, 'gk', cos, sin, cskey, tp['B'][:, 0:256], 'tB', True)
            if k_out is not None:
                st('tB', k_out, tp['B'][:, 0:256], r=['tB'])
            act(kb16[:, :], tp['B'][:, 0:256], AF.Copy, r=['tB'], w=['kb16'])
            for kc in range(2):
                tr(psb[:, 1, kc * 128:(kc + 1) * 128], kb16[:, kc * 128:(kc + 1) * 128], r=['kb16'], w=[('b1', 0), ('b1', 1)])
            act(kt_dst, psb[:, 1, 0:256].rearrange("p (c t) -> p c t", c=2), AF.Copy, r=[('b1', 0), ('b1', 1)], w=[kt_key])
            if v_out is not None:
                act(tp['B'][:, 256:512], psf[:, bkv, 256:512], AF.Copy, r=[('f', bkv)], w=['tBv'])
                st('tBv', v_out, tp['B'][:, 256:512], r=['tBv'])
            S.op('dve', lambda e: e.tensor_copy(out=v_dst.rearrange("p (g d) -> p g d", d=65)[:, :, 0:64],
                                                in_=psf[:, bkv, 256:512].rearrange("p (g d) -> p g d", d=64)),
                 r=[('f', bkv)], w=[v_key])
            norm_rope(psf[:, bki, 0:64], ('f', bki), 1, None, None, cos, sin, cskey, tp['C'][:, 0:64], 'tC', False)
            if ki_out is not None:
                st('tC', ki_out, tp['C'][:, 0:64], r=['tC'])
            S.op('dve', lambda e: e.tensor_copy(out=kidup[:, 0, :].rearrange("p (a d) -> p a d", a=2),
                                                in_=bc(tp['C'][:, 0:64], 1, [128, 2, 64])), r=['tC'], w=['kidup'])
            tr(psb[:, 1, 256:384], kidup[:, 0, :], r=['kidup'], w=['b1r'])
            act(kiT_dst, psb[:, 1, 256:384], AF.Copy, r=['b1r'], w=[kiT_key])

        def q_post(cos, sin, cskey, ws_T, wsk, bs_t, bsk, vn_out, gm_dst, gm_key):
            norm_rope(psf[:, 2, :], ('f', 2), 8, gq, 'gq', cos, sin, cskey, tp['D'][:, :], 'tD', True)
            act(qb[:, :], tp['D'][:, :], AF.Copy, r=['tD'], w=['qb'])
            for c in range(4):
                tr(psb[:, 1, 512 + c * 128:512 + (c + 1) * 128], qb[:, c * 128:(c + 1) * 128], r=['qb'], w=['b1q'])
            QTv = QTz[:, :].rearrange("p (G e r q) -> p G e r q", G=2, e=2, r=2)
            pv = psb[:, 1, 512:1024].rearrange("p (G r q) -> p G r q", G=2, r=2)
            act(QTv[0:64, :, 0, :, :], pv[0:64], AF.Copy, r=['b1q'], w=['QTz'])
            S.op('dve', lambda e: e.tensor_copy(out=QTv[64:128, :, 1, :, :], in_=pv[64:128]), r=['b1q'], w=['QTz'])
            norm_rope(psf[:, 3, :], ('f', 3), 8, None, None, cos, sin, cskey, tp['E'][:, :], 'tD', False)
            act(sm[:, 32:40], psf[:, 1, 64:72], AF.Abs, r=[('f', 1)], w=['sm32'], scale=512.0 ** -0.5)
            ts('dve', sgn[:, :], psf[:, 1, 64:72], 0.0, 2.0, ALU.is_ge, ALU.mult, r=[('f', 1)], w=['sgn'])
            ts('dve', sgn[:, :], sgn[:, :], -1.0, None, ALU.add, None, r=['sgn'], w=['sgn'])
            tt('dve', qib[:, :].rearrange("p (h d) -> p h d", d=64), tp['E'][:, :].rearrange("p (h d) -> p h d", d=64),
               bc(sm[:, 32:40], 2, [128, 8, 64]), ALU.mult, r=['tD', 'sm32'], w=['qib'])
            for c in range(4):
                tr(psb[:, 0, c * 128:(c + 1) * 128], qib[:, c * 128:(c + 1) * 128], r=['qib'], w=['b0'])
            qiv = qiTz[:, :].rearrange("p (c e q) -> p c e q", c=4, e=2)
            pv2 = psb[:, 0, 0:512].rearrange("p (c q) -> p c q", c=4)
            act(qiv[0:64, :, 0, :], pv2[0:64], AF.Copy, r=['b0'], w=['qiTz'])
            S.op('dve', lambda e: e.tensor_copy(out=qiv[64:128, :, 1, :], in_=pv2[64:128]), r=['b0'], w=['qiTz'])
            act(tp['F'][:, :], psf[:, 4, :], AF.Silu, r=[('f', 4)], w=['tF'])
            act(tp['G'][:, :], psf[:, 5, :], AF.Gelu_apprx_tanh, r=[('f', 5)], w=['tG'])
            proj(0, 2632, 512)
            proj(1, 3144, 512)
            act(tp['H'][:, :], psf[:, 0, :], AF.Gelu_apprx_tanh, r=[('f', 0)], w=['tH', 'sm40'], accum_out=sm[:, 40:41])
            act(junk[:, 0:512], tp['H'][:, :], AF.Square, r=['tH'], w=['junk', 'sm41'], accum_out=sm[:, 41:42])
            ts('dve', sm[:, 42:43], sm[:, 40:41], 1.0 / 512, None, ALU.mult, None, r=['sm40'], w=['sm42'])
            tt('dve', sm[:, 43:44], sm[:, 42:43], sm[:, 42:43], ALU.mult, r=['sm42'], w=['sm43'])
            stt(sm[:, 44:45], sm[:, 41:42], 1.0 / 512, sm[:, 43:44], ALU.mult, ALU.subtract, r=['sm41', 'sm43'], w=['sm44'])
            act(sm[:, 45:46], sm[:, 44:45], AF.Sqrt, r=['sm44'], w=['sm45'], scale=1.0, bias=epsb[:, 0:1])
            S.op('dve', lambda e: e.reciprocal(out=sm[:, 46:47], in_=sm[:, 45:46]), r=['sm45'], w=['sm46'])
            ts('dve', tp['H'][:, :], tp['H'][:, :], sm[:, 42:43], sm[:, 46:47], ALU.subtract, ALU.mult, r=['tH', 'sm42', 'sm46'], w=['tH'])
            tt('dve', tp['H'][:, :], tp['H'][:, :], lng[:, :], ALU.mult, r=['tH', 'lng'], w=['tH'])
            tt('pool', tp['H'][:, :], tp['H'][:, :], lnb[:, :], ALU.add, r=['tH', 'lnb'], w=['tH'])
            if vn_out is not None:
                st('tH', vn_out, tp['H'][:, :], r=['tH'])
            act(vnb[:, :], tp['H'][:, :], AF.Copy, r=['tH'], w=['vnb'])
            for g in range(8):
                mm(psf[:, 2, g * 64:(g + 1) * 64], ws_T[:, g, :], vnb[:, g * 64:(g + 1) * 64], True, True,
                   r=['vnb', wsk], w=[('f', 2)])
            tt('dve', tp['A'][:, :].rearrange("p (g d) -> p g d", d=64), psf[:, 2, :].rearrange("p (g d) -> p g d", d=64),
               bc(bs_t[:, :], 2, [128, 8, 64]), ALU.add, r=[('f', 2), bsk], w=['tA'])
            act(tp['C'][:, :], psf[:, 1, :], AF.Silu, r=[('f', 1)], w=['tC'])
            tt('dve', tp['A'][:, :], tp['A'][:, :], tp['G'][:, :], ALU.mult, r=['tA', 'tG'], w=['tA'])
            tt('dve', gm_dst, tp['A'][:, :], tp['C'][:, :], ALU.mult, r=['tA', 'tC'], w=[gm_key])

        def full_proj():
            proj(0, 512, 512)
            proj(1, 1536, 72)
            proj(2, 0, 512)
            proj(3, 1024, 512)
            proj(4, 1608, 512)
            proj(5, 2120, 512)

        def p1a_front(blk):
            x_ = xt[blk % 2]
            xk = ('xt', blk % 2)
            ld(xk, x_[:, :], xb[blk * 128:(blk + 1) * 128, :], w=[xk])
            norm_hT(x_[:, :], xk)
            proj(2 * (blk % 2), 512, 512)
            proj(2 * (blk % 2) + 1, 1536, 64)

        def p1a_back(blk):
            u = blk % 4
            kv_post(cosK[:, blk, :], sinK[:, blk, :], 'cosK', None, None, None,
                    ktb4[:, :, u * 128:(u + 1) * 128], 'ktb4', vaug4[:, u, :], 'vaug4',
                    kiT[:, blk * 128:(blk + 1) * 128], ('kiT', blk // 4), bkv=2 * (blk % 2), bki=2 * (blk % 2) + 1)
            if u == 3:
                c4 = blk // 4
                st('ktb4', KT_s[:, :, c4 * 512:(c4 + 1) * 512], ktb4, r=['ktb4'], w=[('KTs', c4)])
                st('vaug4', V_s[:, c4 * 4:(c4 + 1) * 4, :], vaug4, r=['vaug4'], w=[('Vs', c4)])

        p1a_front(0)
        for blk in range(1, NBLK):
            p1a_front(blk)
            p1a_back(blk - 1)
        p1a_back(NBLK - 1)

        for i in range(NOWN):
            x_ = xt[i % 2]
            xk = ('xt', i % 2)
            ld(xk, x_[:, :], xown[i * 128:(i + 1) * 128, :], w=[xk])
            norm_hT(x_[:, :], xk)
            full_proj()
            rows = slice(i * 128, (i + 1) * 128)
            kv_post(cosQ[:, i, :], sinQ[:, i, :], 'cosK', k_own[rows, :], v_own[rows, :], ki_own[rows, :],
                    KTn[:, :, :], 'KTn', Vn[:, :], 'Vn', kiTn[:, :], 'kiTn')
            q_post(cosQ[:, i, :], sinQ[:, i, :], 'cosK', wsT, 'wsT', bsT, 'bsT', None, mix[:, 512:1024], 'mixg')
            st('QTz', Q_s[i], QTz[:, :], r=['QTz'], w=[('Qs', i)])
            st('qiTz', QI_s[i], qiTz[:, :], r=['qiTz'], w=[('QIs', i)])
            st('sgn', SG_s[i], sgn[:, :], r=['sgn'], w=[('SGs', i)])
            st('tF', GA_s[i], tp['F'][:, :], r=['tF'], w=[('GAs', i)])
            st('mixg', GM_s[i], mix[:, 512:1024], r=['mixg'], w=[('GMs', i)])

        S.barrier()

        for c in range(8):
            ld("tA", tp['A'][:, :], w_out[c * 128:(c + 1) * 128, 0:512], w=['tA'])
            ld("tB", tp['B'][:, :], w_out[c * 128:(c + 1) * 128, 512:1024], w=['tB'])
            act(wout[:, c, 0:512], tp['A'][:, :], AF.Copy, r=['tA'], w=['wout'])
            S.op('dve', lambda e, c=c: e.tensor_copy(out=wout[:, c, 512:1024], in_=tp['B'][:, :]), r=['tB'], w=['wout'])
            ld("tC", tp['C'][:, :], w_gate[c * 128:(c + 1) * 128, 0:512], w=['tC'])
            ld("tD", tp['D'][:, :], w_gate[c * 128:(c + 1) * 128, 512:1024], w=['tD'])
            act(wgate[:, c, 0:512], tp['C'][:, :], AF.Copy, r=['tC', 'gcol'], w=['wgate'], scale=gcol[:, 8 + c:9 + c])
            ts('dve', wgate[:, c, 512:1024], tp['D'][:, :], gcol[:, 8 + c:9 + c], None, ALU.mult, None, r=['tD', 'gcol'], w=['wgate'])
        for c in range(2):
            ld("tA", tp['A'][:, :], w_proj[c * 128:(c + 1) * 128, 0:512], w=['tA'])
            ld("tB", tp['B'][:, :], w_proj[c * 128:(c + 1) * 128, 512:1024], w=['tB'])
            act(wproj[:, c, 0:512], tp['A'][:, :], AF.Copy, r=['tA'], w=['wproj'])
            S.op('dve', lambda e, c=c: e.tensor_copy(out=wproj[:, c, 512:1024], in_=tp['B'][:, :]), r=['tB'], w=['wproj'])

        def score_fma(h, bank, dst, dkey, n):
            rl = tp['A'] if h % 2 == 0 else tp['B']
            rk = 'tA' if h % 2 == 0 else 'tB'
            act(rl[:, 0:n], psf[:, bank, 0:n], AF.Relu, r=[('f', bank)], w=[rk])
            if h == 0:
                ts('dve', dst, rl[:, 0:n], sgn[:, 0:1], None, ALU.mult, None, r=[rk, 'sgn'], w=[dkey])
            else:
                stt(dst, rl[:, 0:n], sgn[:, h:h + 1], dst, ALU.mult, ALU.add, r=[rk, 'sgn', dkey], w=[dkey])

        def bisect(L, mask_ap, mask_key, mcol0, mn_):
            allk = [('sc', c) for c in range((L + 511) // 512)]
            S.op('dve', lambda e: e.tensor_reduce(out=bis[:, 0:1], in_=scores[:, 0:L], axis=AX.X, op=ALU.min), r=allk, w=['bis0'])
            tt('dve', scores[:, mcol0:mcol0 + mn_], scores[:, mcol0:mcol0 + mn_], mask_ap, ALU.add, r=allk + [mask_key], w=allk)
            S.op('dve', lambda e: e.tensor_reduce(out=bis[:, 1:2], in_=scores[:, 0:L], axis=AX.X, op=ALU.max), r=allk, w=['bis1'])
            tt('dve', bis[:, 6:7], bis[:, 1:2], bis[:, 0:1], ALU.subtract, r=['bis0', 'bis1'], w=['bis6'])
            ts('dve', steps[:, :], pow2[:, :], bis[:, 6:7], None, ALU.mult, None, r=['pow2', 'bis6'], w=['steps'])
            tt('dve', bis[:, 2:3], bis[:, 0:1], steps[:, 0:1], ALU.add, r=['bis0', 'steps'], w=['bis2'])
            npieces = (L + 1023) // 1024
            pieces = [(pc * 1024, min(L, pc * 1024 + 1024)) for pc in range(npieces)]
            dpieces = [p_ for k_, p_ in enumerate(pieces) if k_ % 2 == 0]
            apieces = [p_ for k_, p_ in enumerate(pieces) if k_ % 2 == 1]
            nA = sum(b_ - a_ for a_, b_ in apieces)
            for rnd in range(NR):
                if apieces:
                    ts('dve', bis[:, 7:8], bis[:, 2:3], -1.0, None, ALU.mult, None, r=['bis2'], w=['bis7'])
                    for k_, (a_, b_) in enumerate(apieces):
                        act(hb[:, 0:b_ - a_], scores[:, a_:b_], AF.Sign, r=allk + ['bis7'], w=['hb', ('cntA', k_)],
                            bias=bis[:, 7:8], scale=1.0, accum_out=sm[:, 32 + k_:33 + k_])
                for k_, (a_, b_) in enumerate(dpieces):
                    if k_ == 0:
                        S.op('dve', lambda e, a_=a_, b_=b_: e.tensor_scalar(out=junk[:, 0:b_ - a_], in0=scores[:, a_:b_], scalar1=bis[:, 2:3],
                                                                         scalar2=None, op0=ALU.is_ge, op1=ALU.add, accum_out=bis[:, 3:4]),
                             r=allk + ['bis2'], w=['junk', 'bis3'])
                    else:
                        S.op('dve', lambda e, a_=a_, b_=b_: e.tensor_scalar(out=junk[:, 0:b_ - a_], in0=scores[:, a_:b_], scalar1=bis[:, 2:3],
                                                                         scalar2=bis[:, 3:4], op0=ALU.is_ge, op1=ALU.add, accum_out=bis[:, 3:4]),
                             r=allk + ['bis2', 'bis3'], w=['junk', 'bis3'])
                if apieces:
                    nap = len(apieces)
                    ck_ = [('cntA', k_) for k_ in range(nap)]
                    if nap > 1:
                        S.op('dve', lambda e, nap=nap: e.tensor_reduce(out=sm[:, 31:32], in_=sm[:, 32:32 + nap], axis=AX.X, op=ALU.add), r=ck_, w=['sm31'])
                        stt(bis[:, 3:4], sm[:, 31:32], 0.5, bis[:, 3:4], ALU.mult, ALU.add, r=['sm31', 'bis3'], w=['bis3'])
                    else:
                        stt(bis[:, 3:4], sm[:, 32:33], 0.5, bis[:, 3:4], ALU.mult, ALU.add, r=ck_ + ['bis3'], w=['bis3'])
                ts('dve', bis[:, 4:5], bis[:, 3:4], TOPK - 0.5 - 0.5 * nA, steps[:, rnd:rnd + 1], ALU.is_ge, ALU.mult, r=['bis3', 'steps'], w=['bis4'])
                stt(bis[:, 2:3], bis[:, 4:5], steps[:, rnd + 1:rnd + 2], bis[:, 2:3], ALU.subtract, ALU.add, r=['bis4', 'steps', 'bis2'], w=['bis2'])
            stt(bis[:, 5:6], steps[:, NR:NR + 1], -1.5, bis[:, 2:3], ALU.mult, ALU.add, r=['bis2', 'steps'], w=['bis5'])

        def finish_common(xtile, xkey, y_dst):
            for c in range(8):
                tr(psb[:, 0, c * 128:(c + 1) * 128], mix[:, c * 128:(c + 1) * 128], r=['mixa', 'mixg'], w=['b0'])
            act(hT[:].rearrange("p c t -> p (c t)"), psb[:, 0, :], AF.Copy, r=['b0'], w=['hT'])
            for hf in range(2):
                proj(hf, hf * 512, 512, wt=wout, wkey='wout')
            r_ = xt[1]
            for hf in range(2):
                tt('dve', r_[:, hf * 512:(hf + 1) * 512], psf[:, hf, :], xtile[:, hf * 512:(hf + 1) * 512], ALU.add,
                   r=[('f', hf), xkey], w=[('xt', 1)])
            norm_hT(r_[:, :], ('xt', 1))
            for hf in range(2):
                proj(2 + hf, hf * 512, 512, wt=wgate, wkey='wgate')
            for hf in range(2):
                act(gate[:, hf * 512:(hf + 1) * 512], psf[:, 2 + hf, :], AF.Sigmoid, r=[('f', 2 + hf)], w=['gate'])
            act(pb16[:, :], ptile[:, :], AF.Copy, r=['ptile'], w=['pb16'])
            for c in range(2):
                tr(psb[:, 1, c * 128:(c + 1) * 128], pb16[:, c * 128:(c + 1) * 128], r=['pb16'], w=[('b1', 0), ('b1', 1)])
            act(pT[:].rearrange("p c t -> p (c t)"), psb[:, 1, 0:256], AF.Copy, r=[('b1', 0), ('b1', 1)], w=['pT'])
            for hf in range(2):
                for c in range(2):
                    mm(psf[:, 4 + hf, :], pT[:, c, :], wproj[:, c, hf * 512:(hf + 1) * 512], c == 0, c == 1,
                       r=['pT', 'wproj'], w=[('f', 4 + hf)])
            for hf in range(2):
                tt('dve', gate[:, hf * 512:(hf + 1) * 512], gate[:, hf * 512:(hf + 1) * 512], psf[:, 4 + hf, :], ALU.mult,
                   r=['gate', ('f', 4 + hf)], w=['gate'])
            tt('pool', gate[:, :], gate[:, :], r_[:, :], ALU.add, r=['gate', ('xt', 1)], w=['gate'])
            st('gate', y_dst, gate[:, :], r=['gate'])

        for i in range(NOWN):
            nkb = 4 * i + 4
            L = nkb * 128
            nch = i + 1
            ld('QTz', QTz[:, :], Q_s[i], r=[('Qs', i)], w=['QTz'])
            ld('qiTz', qiTz[:, :], QI_s[i], r=[('QIs', i)], w=['qiTz'])
            ld('sgn', sgn[:, :], SG_s[i], r=[('SGs', i)], w=['sgn'])
            ld('tF', tp['F'][:, :], GA_s[i], r=[('GAs', i)], w=['tF'])
            ld('mixg', mix[:, 512:1024], GM_s[i], r=[('GMs', i)], w=['mixg'])
            ld(('xt', 0), xt[0][:, :], xown[i * 128:(i + 1) * 128, :], w=[('xt', 0)])
            ld('ptile', ptile[:, :], pown[i * 128:(i + 1) * 128, :], w=['ptile'])
            qv = qiTz[:, :].rearrange("p (h q) -> p h q", h=8)
            for c in range(nch):
                for h in range(8):
                    bank = (c * 8 + h) % 4
                    mm(psf[:, bank, :], qv[:, h, :], kiT[:, c * 512:(c + 1) * 512], True, True,
                       r=['qiTz', ('kiT', c)], w=[('f', bank)])
                    score_fma(h, bank, scores[:, c * 512:(c + 1) * 512], ('sc', c), 512)
            bisect(L, cmask[:, :], 'cmask', L - 512, 512)
            Qv = QTz[:, :].rearrange("p (g n) -> p g n", g=4)

            def ldchunk(c4):
                sl = c4 % 2
                ld(('KTc', sl), KTc[sl][:, :, :], KT_s[:, :, c4 * 512:(c4 + 1) * 512], r=[('KTs', c4)], w=[('KTc', sl)])
                ld(('Vc', sl), Vc[sl][:, :, :], V_s[:, c4 * 4:(c4 + 1) * 4, :], r=[('Vs', c4)], w=[('Vc', sl)])

            def att_A(kb):
                c4 = kb // 4
                u = kb % 4
                sl = c4 % 2
                p2 = kb % 2
                ts('dve', mk[p2][:, :], scores[:, kb * 128:(kb + 1) * 128], bis[:, 5:6], None, ALU.is_ge, None,
                   r=[('sc', c4), 'bis5'], w=[('mk', p2)])
                tr(psb[:, 1, p2 * 128:(p2 + 1) * 128], mk[p2][:, :], r=[('mk', p2)], w=[('b1', p2)])
                for g in range(4):
                    bank = 2 * p2 + g // 2
                    mm(psf[:, bank, (g % 2) * 256:(g % 2) * 256 + 256], KTc[sl][:, g // 2, u * 128:(u + 1) * 128], Qv[:, g, :], True, True,
                       r=[('KTc', sl), 'QTz'], w=[('f', bank)])
                for hf in range(2):
                    act(PT[p2][:, hf * 512:(hf + 1) * 512], psf[:, 2 * p2 + hf, :], AF.Exp, r=[('f', 2 * p2 + hf)], w=[('PT', p2, hf)], scale=0.125)
                    tt('dve', PT[p2][:, hf * 512:(hf + 1) * 512].rearrange("p (h q) -> p h q", h=4),
                       PT[p2][:, hf * 512:(hf + 1) * 512].rearrange("p (h q) -> p h q", h=4),
                       bc(psb[:, 1, p2 * 128:(p2 + 1) * 128], 1, [128, 4, 128]), ALU.mult, r=[('PT', p2, hf), ('b1', p2)], w=[('PT', p2, hf)])

            def att_B(kb):
                c4 = kb // 4
                u = kb % 4
                sl = c4 % 2
                p2 = kb % 2
                for h in range(8):
                    bank = 4 + h // 4
                    mm(psf[:, bank, (h % 4) * 65:(h % 4) * 65 + 65], PT[p2][:, h * 128:(h + 1) * 128],
                       Vc[sl][:, u, (h // 2) * 65:(h // 2) * 65 + 65], kb == 0 and h % 4 == 0, kb == nkb - 1 and h % 4 == 3,
                       r=[('PT', p2, h // 4), ('Vc', sl)], w=[('f', bank)])
                if u == 3 and c4 + 2 < nch:
                    ldchunk(c4 + 2)

            ldchunk(0)
            if nch > 1:
                ldchunk(1)
            att_A(0)
            for kb in range(1, nkb):
                att_A(kb)
                att_B(kb - 1)
            att_B(nkb - 1)
            for hf in range(2):
                o3 = psf[:, 4 + hf, 0:260].rearrange("p (h d) -> p h d", d=65)
                S.op('dve', lambda e, o3=o3, hf=hf: e.reciprocal(out=sm[:, 48 + 4 * hf:52 + 4 * hf].unsqueeze(2), in_=o3[:, :, 64:65]),
                     r=[('f', 4 + hf)], w=['sm48'])
                tt('dve', tp['C'][:, hf * 256:(hf + 1) * 256].rearrange("p (h d) -> p h d", d=64), o3[:, :, 0:64],
                   bc(sm[:, 48 + 4 * hf:52 + 4 * hf], 2, [128, 4, 64]), ALU.mult, r=[('f', 4 + hf), 'sm48'], w=['tC'])
            tt('dve', mix[:, 0:512], tp['C'][:, :], tp['F'][:, :], ALU.mult, r=['tC', 'tF'], w=['mixa'])
            finish_common(xt[0], ('xt', 0), y_own[i * 128:(i + 1) * 128, :])

        S.barrier()

        for c in range(8):
            for (a, b_) in ((0, 1024), (1024, 2048), (2048, 3072), (3072, DIN)):
                n_ = b_ - a
                ld("gate", gate[:, 0:n_], w_in[c * 128:(c + 1) * 128, a:b_], w=['gate'])
                if a == 0:
                    for G_ in range(2):
                        ts('dve', win[:, c, G_ * 256:(G_ + 1) * 256].rearrange("p (r e d) -> p r e d", r=2, e=2),
                           gate[:, G_ * 256:(G_ + 1) * 256].rearrange("p (e r d) -> p r e d", e=2, r=2),
                           gcol[:, c:c + 1], None, ALU.mult, None, r=['gate', 'gcol'], w=['win'])
                    act(win[:, c, 512:1024], gate[:, 512:1024], AF.Copy, r=['gate', 'gcol'], w=['win'], scale=gcol[:, c:c + 1])
                else:
                    act(win[:, c, a:b_], gate[:, 0:n_], AF.Copy, r=['gate', 'gcol'], w=['win'], scale=gcol[:, c:c + 1])
        ld(('xt', 0), xt[0][:, :], xs, w=[('xt', 0)])
        ld('ptile', ptile[:, :], ps_d, w=['ptile'])
        norm_hT(xt[0][:, :], ('xt', 0))
        full_proj()
        kv_post(cosS[:, :], sinS[:, :], 'cosS', ks, vs, kis, KTn[:, :, :], 'KTn', Vn[:, :], 'Vn', kiTn[:, :], 'kiTn')
        q_post(cosS[:, :], sinS[:, :], 'cosS', wsTS, 'wsTS', bsS, 'bsS', vns, mix[:, 512:1024], 'mixg')
        S.barrier()

        for i in range(2):
            S.op('pool', lambda e, i=i: e.memset(Vp[i], 1.0), w=[('Vp', i)])
        ld(('xt', 1), xt[1][:, :].bitcast(I32), pt_d.rearrange("(o s) n -> o (s n)", o=1).partition_broadcast(128), w=[('xt', 1)])
        idx4 = gate[:, 0:256].bitcast(I32).rearrange("p (s c) -> p s c", s=16)
        for q4 in range(4):
            ps_ = slice(32 * q4, 32 * q4 + 32)
            ts('dve', idx4[ps_], ptb[ps_].rearrange("p s (c f) -> p s c f", f=4)[:, :, :, q4], 32.0, piof[ps_, 0:1], ALU.mult, ALU.add,
               r=[('xt', 1), 'piof'], w=['gate'])
        S.op('pool', lambda e: e.memset(qiTzS, 0.0), w=['qiTzS'])
        qv = qiTz[:, :].rearrange("p (h q) -> p h q", h=8)
        for s_ in range(16):
            S.op('dve', lambda e, s_=s_: e.tensor_copy(out=qiTzS[:, s_, :, 8 * s_:8 * s_ + 8], in_=qv[:, :, 8 * s_:8 * s_ + 8]),
                 r=['qiTz', 'qiTzS'], w=['qiTzS'])
        for c in range(16):
            half = c % 2
            for s_ in range(16):
                sl = s_ % 2
                S.dma('pool', ('kraw', sl), lambda e, s_=s_, sl=sl, c=c: e.indirect_dma_start(
                    out=kraw[sl].rearrange("p u d -> p (u d)"), out_offset=None, in_=cik,
                    in_offset=bass.IndirectOffsetOnAxis(ap=idx4[:, s_, c:c + 1], axis=0)),
                    r=['gate'], w=[('kraw', sl)])
                S.op('dve', lambda e, sl=sl: e.tensor_copy(out=kidup[:, :, :].rearrange("p u (a d) -> p u a d", a=2),
                                                          in_=bc(kraw[sl], 2, [128, 4, 2, 64])),
                     r=[('kraw', sl)], w=['kidup'])
                bkeys = ['b0'] if sl == 0 else [('b1', 0), ('b1', 1), 'b1r']
                for u in range(4):
                    tr(psb[:, sl, u * 128:(u + 1) * 128], kidup[:, u, :], r=['kidup'], w=bkeys)
                act(kiTc[:, half * 16 + s_, :], psb[:, sl, 0:512], AF.Copy, r=bkeys, w=[('kiTc', half)])
            for h in range(8):
                bank = h % 4
                for s_ in range(16):
                    mm(psf[:, bank, :], qiTzS[:, s_, h, :], kiTc[:, half * 16 + s_, :], s_ == 0, s_ == 15,
                       r=['qiTzS', ('kiTc', half)], w=[('f', bank)])
                score_fma(h, bank, scores[:, c * 512:(c + 1) * 512], ('sc', c), 512)
        for h in range(8):
            bank = h % 4
            mm(psf[:, bank, 0:128], qv[:, h, :], kiTn[:, :], True, True, r=['qiTz', 'kiTn'], w=[('f', bank)])
            score_fma(h, bank, scores[:, PAST:PAST + 128], ('sc', 16), 128)
        bisect(LS, newneg[:, :], 'newneg', PAST, 128)
        Qv5 = QTz[:, :].rearrange("p (g r q) -> p g r q", g=4, r=2)
        Kg = [big[:, LS + 1024 * i:LS + 1024 * (i + 1)] for i in range(2)]
        Vg = [big[:, LS + 2048 + 1024 * i:LS + 2048 + 1024 * (i + 1)] for i in range(2)]

        def gather_chunk(k):
            s2, c2 = divmod(k, 16)
            slc = k % 2
            S.dma('pool', ('Kg', slc), lambda e: e.indirect_dma_start(
                out=Kg[slc], out_offset=None, in_=ck,
                in_offset=bass.IndirectOffsetOnAxis(ap=idx4[:, s2, c2:c2 + 1], axis=0)), r=['gate', 'bis5'], w=[('Kg', slc)])
            S.dma('pool', ('Vg', slc), lambda e: e.indirect_dma_start(
                out=Vg[slc], out_offset=None, in_=cv,
                in_offset=bass.IndirectOffsetOnAxis(ap=idx4[:, s2, c2:c2 + 1], axis=0)), r=['gate', 'bis5'], w=[('Vg', slc)])

        def satt_A(s_, pg):
            it = s_ * (NPG + 1) + pg
            sl = it % 2
            if pg < NPG:
                k = s_ * 16 + pg // 4
                j = pg % 4
                slc = k % 2
                if j == 0 and k + 1 < 256:
                    gather_chunk(k + 1)
                act(kb16[:, :], Kg[slc][:, j * 256:(j + 1) * 256], AF.Copy, r=[('Kg', slc)], w=['kb16'])
                for kc in range(2):
                    tr(psb[:, 0, kc * 128:(kc + 1) * 128], kb16[:, kc * 128:(kc + 1) * 128], r=['kb16'], w=['b0'])
                S.op('dve', lambda e: e.tensor_copy(out=KTp[sl], in_=psb[:, 0, 0:256].rearrange("p (c t) -> p c t", c=2)),
                     r=['b0'], w=[('KTp', sl)])
                act(Vp[sl].rearrange("p (g d) -> p g d", d=65)[:, :, 0:64], Vg[slc][:, j * 256:(j + 1) * 256].rearrange("p (g d) -> p g d", d=64),
                    AF.Copy, r=[('Vg', slc)], w=[('Vp', sl)])
                KT_ = KTp[sl]
                ktk = ('KTp', sl)
            else:
                KT_ = KTn
                ktk = 'KTn'
            ts('dve', mk[sl][:, :], scores[:, pg * 128:(pg + 1) * 128], bis[:, 5:6], None, ALU.is_ge, None,
               r=[('sc', pg // 4), 'bis5'], w=[('mk', sl)])
            tr(psb[:, 1, sl * 128:(sl + 1) * 128], mk[sl][:, :], r=[('mk', sl)], w=[('b1', sl)])
            for g in range(4):
                mm(psf[:, sl, g * 16:(g + 1) * 16], KT_[:, g // 2, :], Qv5[:, g, :, 8 * s_:8 * s_ + 8], True, True,
                   r=[ktk, 'QTz'], w=[('f', sl)])
            act(PTs[sl][:, :], psf[:, sl, 0:64], AF.Exp, r=[('f', sl)], w=[('PTs', sl)], scale=0.125)
            tt('dve', PTs[sl][:, :].rearrange("p (h t) -> p h t", h=8), PTs[sl][:, :].rearrange("p (h t) -> p h t", h=8),
               bc(psb[:, 1, sl * 128 + 8 * s_:sl * 128 + 8 * s_ + 8], 1, [128, 8, 8]), ALU.mult,
               r=[('PTs', sl), ('b1', sl)], w=[('PTs', sl)])

        def satt_B(s_, pg):
            it = s_ * (NPG + 1) + pg
            sl = it % 2
            if pg < NPG:
                V_ = Vp[sl]
                vk = ('Vp', sl)
            else:
                V_ = Vn
                vk = 'Vn'
            for g in range(4):
                mm(psf[0:16, 4, g * 65:(g + 1) * 65], PTs[sl][:, g * 16:(g + 1) * 16], V_[:, g * 65:(g + 1) * 65],
                   pg == 0 and g == 0, pg == NPG and g == 3, r=[('PTs', sl), vk], w=[('f', 4)])
            if pg == NPG:
                o3 = psf[0:16, 4, 0:260].rearrange("p (g d) -> p g d", d=65)
                S.op('dve', lambda e: e.reciprocal(out=sm[0:16, 56:60].unsqueeze(2), in_=o3[:, :, 64:65]), r=[('f', 4)], w=['sm56'])
                tt('dve', On.rearrange("p (g d) -> p g d", d=64), o3[:, :, 0:64], bc(sm[0:16, 56:60], 2, [16, 4, 64]), ALU.mult,
                   r=[('f', 4), 'sm56'], w=['On'])
                On2v = On2.rearrange("p (g r d) -> p g r d", g=4, r=2)
                for r_i in range(2):
                    ts('dve', On2v[:, :, r_i, :], On.rearrange("p (g d) -> p g d", d=64), m01[:, r_i:r_i + 1], None, ALU.mult, None,
                       r=['On', 'm01'], w=['On2'])
                mm(psf[:, 5, :], selb[:, 120 - 8 * s_:248 - 8 * s_], On2, s_ == 0, s_ == 15, r=['selb', 'On2'], w=[('f', 5)])

        its = [(s_, pg) for s_ in range(16) for pg in range(NPG + 1)]
        gather_chunk(0)
        satt_A(*its[0])
        for k_ in range(1, len(its)):
            satt_A(*its[k_])
            satt_B(*its[k_ - 1])
        satt_B(*its[-1])
        tt('dve', mix[:, 0:512], psf[:, 5, :], tp['F'][:, :], ALU.mult, r=[('f', 5), 'tF'], w=['mixa'])
        finish_common(xt[0], ('xt', 0), ys)

        S.barrier()
    return nc


_NC_CACHE = {}


def _consts(j):
    c = {}
    c["identb"] = np.eye(128, dtype=np.float32).astype(ml_dtypes.bfloat16)
    ii = np.arange(128)[:, None]
    jj = np.arange(128)[None, :]
    c["tril"] = (jj <= ii).astype(np.float32)
    tri_neg = np.where(jj <= ii, 0.0, NEG).astype(np.float32)
    cm = np.zeros((128, 4, 128), np.float32)
    for u in range(4):
        if u == j:
            cm[:, u, :] = tri_neg
        elif u > j:
            cm[:, u, :] = NEG
    c["cmask"] = cm.reshape(128, 512)
    same = (ii // 8) == (jj // 8)
    c["newneg"] = np.where(same & ((jj % 8) <= (ii % 8)), 0.0, NEG).astype(np.float32)
    half = 8
    inv = np.power(np.float32(500000.0), -np.arange(half, dtype=np.float32) * np.float32(2.0) / np.float32(16))

    def cs(pos):
        ang = pos.astype(np.float32)[..., None] * inv
        return np.cos(ang).astype(np.float32), np.sin(ang).astype(np.float32)
    posK = (np.arange(NBLK)[None, :] * 128 + np.arange(128)[:, None])
    c["cosK"], c["sinK"] = cs(posK)
    posQ = ((np.arange(NOWN)[None, :] * 4 + j) * 128 + np.arange(128)[:, None])
    c["cosQ"], c["sinQ"] = cs(posQ)
    posS = PAST + (np.arange(128) % 8)
    c["cosS"], c["sinS"] = cs(posS)
    selb = np.zeros((16, 248), np.float32)
    for r in range(2):
        for t in range(8):
            selb[r * 8 + t, 120 + t] = 1.0
    c["selb"] = selb
    m01 = np.zeros((16, 2), np.float32)
    m01[0:8, 0] = 1.0
    m01[8:16, 1] = 1.0
    c["m01"] = m01
    c["pow2"] = np.tile((2.0 ** -(np.arange(NR + 2) + 1.0)).astype(np.float32)[None, :], (128, 1))
    c["piof"] = (np.arange(128) % 32).astype(np.float32)[:, None]
    return c


def kernel(x_prompt, x_sample, cache_k, cache_v, cache_idx_k, page_table, p_prompt, p_sample,
           norm_in_g, w_in, q_norm_g, k_norm_g, ln_v_g, ln_v_b, w_s, b_s, w_out,
           ple_norm_g, w_ple_gate, w_ple_proj):
    f = lambda a: np.ascontiguousarray(np.asarray(a))
    x_prompt, x_sample, p_prompt, p_sample = f(x_prompt), f(x_sample), f(p_prompt), f(p_sample)
    ck = f(cache_k).reshape(NPOOL * 32, 1024)
    cvv = f(cache_v).reshape(NPOOL * 32, 1024)
    cik = f(cache_idx_k).reshape(NPOOL * 32, 256)
    page_table = f(page_table).astype(np.int32)
    w_s = f(w_s)[0]
    b_s = f(b_s)[0]
    if "nc" not in _NC_CACHE:
        _NC_CACHE["nc"] = build()
    nc = _NC_CACHE["nc"]
    wsS = np.zeros((128, 8, 128), np.float32)
    for s_ in range(16):
        wsS[8 * s_:8 * s_ + 8, :, 8 * s_:8 * s_ + 8] = np.transpose(w_s[:, 0:8, 0:8], (1, 0, 2))
    shared = {
        "ck": ck, "cv": cvv, "cik": cik,
        "w_in": f(w_in)[0], "w_out": f(w_out)[0], "w_gate": f(w_ple_gate)[0], "w_proj": f(w_ple_proj)[0],
        "gin_col": f(f(norm_in_g)[0].reshape(8, 128).T), "gple_col": f(f(ple_norm_g)[0].reshape(8, 128).T),
        "gq": f(q_norm_g).reshape(1, 64), "gk": f(k_norm_g).reshape(1, 64),
        "lng": f(ln_v_g).reshape(1, 512), "lnb": f(ln_v_b).reshape(1, 512),
        "ws": f(np.transpose(w_s, (1, 0, 2))), "wsS": wsS,
        "bsT": f(b_s.T), "bsS": f(np.tile(b_s[:, 0:8].T, (16, 1))),
    }
    in_maps = []
    for c in range(8):
        b, j = c // 4, c % 4
        m = dict(shared)
        m.update(_consts(j))
        m["xb"] = x_prompt[b]
        m["xown"] = f(x_prompt[b].reshape(NBLK, 128, D)[j::4].reshape(NOWN * 128, D))
        m["pown"] = f(p_prompt[0, b].reshape(NBLK, 128, 256)[j::4].reshape(NOWN * 128, 256))
        m["xs"] = f(x_sample[16 * c:16 * c + 16].reshape(128, D))
        m["ps"] = f(p_sample[0, 16 * c:16 * c + 16].reshape(128, 256))
        m["pt"] = f(page_table[16 * c:16 * c + 16])
        in_maps.append(m)
    res = run_bass_kernel_spmd(nc, in_maps, core_ids=list(range(8))).results

    def own(name, width):
        out = np.zeros((2, NBLK, 128, width), np.float32)
        for c in range(8):
            b, j = c // 4, c % 4
            out[b, j::4] = res[c][name].reshape(NOWN, 128, width)
        return out.reshape(2, SEQ, width)
    y_prompt = own("y_own", D)
    k_prompt = own("k_own", 256).reshape(1, 2, SEQ, 4, 64)
    v_prompt = own("v_own", 256).reshape(1, 2, SEQ, 4, 64)
    ik_prompt = own("ki_own", 64).reshape(1, 2, SEQ, 64)
    cat = lambda name, shp: np.concatenate([res[c][name] for c in range(8)], axis=0).reshape(shp)
    y_sample = cat("ys", (128, 8, D))
    k_sample = cat("ks", (1, 128, 8, 4, 64))
    v_sample = cat("vs", (1, 128, 8, 4, 64))
    ik_sample = cat("kis", (1, 128, 8, 64))
    vn_sample = cat("vns", (1, 128, 8, 512))
    return (y_prompt, y_sample, k_prompt, v_prompt, ik_prompt, k_sample, v_sample, ik_sample, vn_sample)
```
